# Optimizing a Trainium2 kernel written in Bass

```python
import math
import jax, jax.numpy as jnp
from jax import lax
import numpy as np

D_MODEL = 1024
BATCH = 8
SEQ = 2048
DEPTH = 4

GRID_W = 64
CTX_LEN = 256
MIX_WIDTH = D_MODEL
NA_HEADS = 6
NA_HEAD_DIM = 64
NA_WIDTH = NA_HEADS * NA_HEAD_DIM
NA_WIN_H = 8
NA_WIN_W = 16
NA_QBLOCK = 16
NA_STRIP = NA_QBLOCK + NA_WIN_W
DIFF_HEADS = 4
DIFF_QK_DIM = 48
DIFF_V_DIM = 2 * DIFF_QK_DIM
DIFF_WIDTH = DIFF_HEADS * DIFF_V_DIM
DIFF_QBLOCK = 128
ROPE_BASE = 10000.0
CONV_WIDTH = MIX_WIDTH - NA_WIDTH - DIFF_WIDTH
CONV_TAPS = 31
COL_SPLITS = (NA_WIDTH, NA_WIDTH, DIFF_WIDTH, DIFF_WIDTH,
              NA_WIDTH, NA_WIDTH, DIFF_WIDTH, DIFF_WIDTH,
              CONV_WIDTH, CONV_WIDTH, CONV_WIDTH)
IN_COLS = 4 * NA_WIDTH + 4 * DIFF_WIDTH + 3 * CONV_WIDTH
EPS = 1e-6
NEG_INF = -1e30

kernel_name = 'hybrid_na_diffattn_conformer_dit_trunk'


def _rms(x, g):
    xf = x.astype(jnp.float32)
    y = xf * lax.rsqrt(jnp.mean(xf * xf, axis=-1, keepdims=True) + EPS)
    return (y * g.astype(jnp.float32)).astype(x.dtype)


def _layer_norm(x, g, b):
    xf = x.astype(jnp.float32)
    mu = jnp.mean(xf, axis=-1, keepdims=True)
    var = jnp.mean(jnp.square(xf - mu), axis=-1, keepdims=True)
    y = (xf - mu) * lax.rsqrt(var + EPS) * g.astype(jnp.float32) + b.astype(jnp.float32)
    return y.astype(x.dtype)


def _heads(z, n_heads):
    return z.reshape(z.shape[0], z.shape[1], n_heads, z.shape[-1] // n_heads)


def _split_cols(z, n):
    idx = np.cumsum(COL_SPLITS)[:n - 1].tolist()
    return jnp.split(z, idx, axis=-1)


def _rope_1d(x, pos):
    d = x.shape[-1]
    inv = ROPE_BASE ** (-jnp.arange(0, d, 2, dtype=jnp.float32) / d)
    ang = pos.astype(jnp.float32)[:, None] * inv[None, :]
    cos, sin = jnp.cos(ang)[:, None, :], jnp.sin(ang)[:, None, :]
    xf = x.astype(jnp.float32)
    x1, x2 = xf[..., :d // 2], xf[..., d // 2:]
    return jnp.concatenate([x1 * cos - x2 * sin, x2 * cos + x1 * sin], axis=-1).astype(x.dtype)


def _rope_2d(x, rows, cols):
    half = x.shape[-1] // 2
    return jnp.concatenate([_rope_1d(x[..., :half], rows), _rope_1d(x[..., half:], cols)], axis=-1)


def _na_tables():
    ncb = GRID_W // NA_QBLOCK
    qcols = np.arange(GRID_W).reshape(ncb, NA_QBLOCK)
    kstart = np.clip(np.arange(ncb) * NA_QBLOCK - NA_WIN_W // 2, 0, GRID_W - NA_STRIP)
    kcols = kstart[:, None] + np.arange(NA_STRIP)
    cstart = np.clip(qcols - NA_WIN_W // 2, 0, GRID_W - NA_WIN_W)
    kc = kcols[:, None, :]
    valid = (kc >= cstart[:, :, None]) & (kc < cstart[:, :, None] + NA_WIN_W)
    dc_idx = np.clip(kc - qcols[:, :, None] + NA_WIN_W - 1, 0, 2 * NA_WIN_W - 2)
    return kcols, valid, dc_idx


def _neighbourhood_attention(q, k, v, kc, vc, rpb):
    B, T, H, dh = q.shape
    n_rows = T // GRID_W
    wr = min(NA_WIN_H, n_rows)
    ncb = GRID_W // NA_QBLOCK
    nk = wr * NA_STRIP
    kcols, valid, dc_idx = _na_tables()
    mask = np.broadcast_to(valid[:, :, None, :], (ncb, NA_QBLOCK, wr, NA_STRIP)).reshape(ncb, NA_QBLOCK, nk)
    qg = (q * NA_HEAD_DIM ** -0.5).reshape(B, n_rows, GRID_W, H, dh)
    kg = k.reshape(B, n_rows, GRID_W, H, dh)
    vg = v.reshape(B, n_rows, GRID_W, H, dh)

    def row(r):
        rs = jnp.clip(r - wr // 2, 0, n_rows - wr)

        def strips(t):
            t = lax.dynamic_slice_in_dim(t, rs, wr, axis=1)[:, :, kcols]
            return t.transpose(0, 2, 1, 3, 4, 5).reshape(B, ncb, nk, H, dh)

        ks_, vs_ = strips(kg), strips(vg)
        qr = lax.dynamic_index_in_dim(qg, r, axis=1, keepdims=False).reshape(B, ncb, NA_QBLOCK, H, dh)
        dr_idx = rs + jnp.arange(wr) - r + (NA_WIN_H - 1)
        bias = rpb[:, dr_idx][:, :, dc_idx].transpose(0, 2, 3, 1, 4).reshape(H, ncb, NA_QBLOCK, nk)
        s_loc = jnp.einsum('bnqhd,bnkhd->bhnqk', qr, ks_).astype(jnp.float32) + bias.astype(jnp.float32)
        s_loc = jnp.where(mask, s_loc, NEG_INF)
        s_ctx = jnp.einsum('bnqhd,bkhd->bhnqk', qr, kc).astype(jnp.float32)
        p = jax.nn.softmax(jnp.concatenate([s_loc, s_ctx], axis=-1), axis=-1).astype(v.dtype)
        o = (jnp.einsum('bhnqk,bnkhd->bnqhd', p[..., :nk], vs_)
             + jnp.einsum('bhnqk,bkhd->bnqhd', p[..., nk:], vc))
        return o.reshape(B, GRID_W, H, dh)

    out = lax.map(row, jnp.arange(n_rows))
    return out.transpose(1, 0, 2, 3, 4).reshape(B, T, H * dh)


def _dense_attend(q, k, v):
    s = jnp.einsum('bqhd,bkhd->bhqk', q, k).astype(jnp.float32) * (q.shape[-1] ** -0.5)
    p = jax.nn.softmax(s, axis=-1).astype(v.dtype)
    return jnp.einsum('bhqk,bkhd->bqhd', p, v)


def _diff_attend(q1, q2, k1, k2, v, lam):
    B, Tq, H, d = q1.shape
    nb = Tq // DIFF_QBLOCK
    scale = d ** -0.5

    def to_blocks(t):
        return t.reshape(B, nb, DIFF_QBLOCK, H, d).swapaxes(0, 1)

    def blk(qs):
        a, b = qs
        s1 = jnp.einsum('bqhd,bkhd->bhqk', a, k1).astype(jnp.float32) * scale
        s2 = jnp.einsum('bqhd,bkhd->bhqk', b, k2).astype(jnp.float32) * scale
        w = jax.nn.softmax(s1, axis=-1) - lam * jax.nn.softmax(s2, axis=-1)
        return jnp.einsum('bhqk,bkhe->bqhe', w.astype(v.dtype), v)

    o = lax.map(blk, (to_blocks(q1), to_blocks(q2)))
    return o.swapaxes(0, 1).reshape(B, Tq, H, v.shape[-1])


def _diff_merge(o, subln_g, lam_init):
    return (_rms(o, subln_g) * (1.0 - lam_init)).reshape(o.shape[0], o.shape[1], -1)


def _conformer_conv(a, b, conv_w, conv_b, ln_g, ln_b):
    u = a * jax.nn.sigmoid(b)
    pad = CONV_TAPS // 2
    y = lax.conv_general_dilated(u, conv_w[:, None, :].astype(u.dtype), (1,), [(pad, pad)],
                                 dimension_numbers=('NWC', 'WIO', 'NWC'),
                                 feature_group_count=CONV_WIDTH) + conv_b
    return jax.nn.silu(_layer_norm(y, ln_g, ln_b))


def _layer(x, xc, c, c_ctx, w_mod, b_mod, g_pre, g_post, w_in, w_out, rpb,
           lq1, lk1, lq2, lk2, subln_g, conv_w, conv_b, ln_g, ln_b, layer_idx, update_ctx):
    B, T, _ = x.shape
    t = jnp.arange(T)
    rows, cols = t // GRID_W, t % GRID_W
    d = DIFF_QK_DIM

    shift, scale, gate = jnp.split((jax.nn.silu(c) @ w_mod + b_mod)[:, None, :], 3, axis=-1)
    n_mod = 3 if update_ctx else 2
    mod_c = jnp.split(jax.nn.silu(c_ctx) @ w_mod[:, :n_mod * D_MODEL] + b_mod[:n_mod * D_MODEL], n_mod)

    h = _rms(x, g_pre) * (1.0 + scale) + shift
    hc = _rms(xc, g_pre) * (1.0 + mod_c[1]) + mod_c[0]

    na_k, na_v, df_k, df_v, na_q, na_g, df_q, df_g, cv_a, cv_b, cv_g = _split_cols(h @ w_in, len(COL_SPLITS))
    n_cols = len(COL_SPLITS) if update_ctx else 4
    zc = _split_cols(hc @ w_in[:, :sum(COL_SPLITS[:n_cols])], n_cols)
    nak_c, nav_c, dfk_c, dfv_c = zc[:4]

    nak_c, nav_c = _heads(nak_c, NA_HEADS), _heads(nav_c, NA_HEADS)
    a_out = _neighbourhood_attention(_heads(na_q, NA_HEADS), _heads(na_k, NA_HEADS), _heads(na_v, NA_HEADS),
                                     nak_c, nav_c, rpb)

    lam_init = 0.8 - 0.6 * math.exp(-0.3 * layer_idx)
    lam = (jnp.exp(jnp.sum(lq1 * lk1).astype(jnp.float32))
           - jnp.exp(jnp.sum(lq2 * lk2).astype(jnp.float32)) + lam_init)
    rope = lambda z: _rope_2d(z, rows, cols)
    q, k = _heads(df_q, DIFF_HEADS), _heads(df_k, DIFF_HEADS)
    kc, vc = _heads(dfk_c, DIFF_HEADS), _heads(dfv_c, DIFF_HEADS)
    k1 = jnp.concatenate([rope(k[..., :d]), kc[..., :d]], axis=1)
    k2 = jnp.concatenate([rope(k[..., d:]), kc[..., d:]], axis=1)
    vd = jnp.concatenate([_heads(df_v, DIFF_HEADS), vc], axis=1)
    b_out = _diff_merge(_diff_attend(rope(q[..., :d]), rope(q[..., d:]), k1, k2, vd, lam), subln_g, lam_init)

    c_out = _conformer_conv(cv_a, cv_b, conv_w, conv_b, ln_g, ln_b)

    y = jnp.concatenate([a_out * jax.nn.silu(na_g), b_out * jax.nn.silu(df_g),
                         c_out * jax.nn.silu(cv_g)], axis=-1) @ w_out
    x_new = x + gate * _rms(y, g_post)

    if update_ctx:
        naq_c, nag_c, dfq_c, dfg_c, cva_c, cvb_c, cvg_c = zc[4:]
        a_c = _dense_attend(_heads(naq_c, NA_HEADS), nak_c, nav_c).reshape(B, -1, NA_WIDTH)
        qc = _heads(dfq_c, DIFF_HEADS)
        b_c = _diff_merge(_diff_attend(qc[..., :d], qc[..., d:], kc[..., :d], kc[..., d:], vc, lam),
                          subln_g, lam_init)
        c_c = _conformer_conv(cva_c, cvb_c, conv_w, conv_b, ln_g, ln_b)
        yc = jnp.concatenate([a_c * jax.nn.silu(nag_c), b_c * jax.nn.silu(dfg_c),
                              c_c * jax.nn.silu(cvg_c)], axis=-1) @ w_out
        xc = xc + mod_c[2] * _rms(yc, g_post)
    return x_new, xc


def setup_inputs(seed: int = 0) -> dict:
    key = jax.random.key(seed)
    ks = jax.random.split(key, 20)
    f32 = jnp.float32

    def nrm(k, shape, s):
        return s * jax.random.normal(k, shape, f32)

    return {
        'x': nrm(ks[0], (BATCH, SEQ, D_MODEL), 1.0),
        'c': nrm(ks[1], (BATCH, D_MODEL), 1.0),
        'ctx': nrm(ks[2], (BATCH, CTX_LEN, D_MODEL), 1.0),
        'c_ctx': nrm(ks[3], (D_MODEL,), 1.0),
        'w_mod': nrm(ks[4], (DEPTH, D_MODEL, 3 * D_MODEL), 0.5 * D_MODEL ** -0.5),
        'b_mod': nrm(ks[5], (DEPTH, 3 * D_MODEL), 0.02),
        'g_pre': 1.0 + nrm(ks[6], (DEPTH, D_MODEL), 0.1),
        'g_post': 1.0 + nrm(ks[7], (DEPTH, D_MODEL), 0.1),
        'w_in': nrm(ks[8], (DEPTH, D_MODEL, IN_COLS), D_MODEL ** -0.5),
        'w_out': nrm(ks[9], (DEPTH, MIX_WIDTH, D_MODEL), MIX_WIDTH ** -0.5),
        'na_rpb': nrm(ks[10], (DEPTH, NA_HEADS, 2 * NA_WIN_H - 1, 2 * NA_WIN_W - 1), 0.1),
        'diff_lq1': nrm(ks[11], (DEPTH, DIFF_QK_DIM), 0.1),
        'diff_lk1': nrm(ks[12], (DEPTH, DIFF_QK_DIM), 0.1),
        'diff_lq2': nrm(ks[13], (DEPTH, DIFF_QK_DIM), 0.1),
        'diff_lk2': nrm(ks[14], (DEPTH, DIFF_QK_DIM), 0.1),
        'diff_subln_g': 1.0 + nrm(ks[15], (DEPTH, DIFF_V_DIM), 0.1),
        'conv_w': nrm(ks[16], (DEPTH, CONV_TAPS, CONV_WIDTH), CONV_TAPS ** -0.5),
        'conv_b': nrm(ks[17], (DEPTH, CONV_WIDTH), 0.02),
        'conv_ln_g': 1.0 + nrm(ks[18], (DEPTH, CONV_WIDTH), 0.1),
        'conv_ln_b': nrm(ks[19], (DEPTH, CONV_WIDTH), 0.02),
    }


def reference(x, c, ctx, c_ctx, w_mod, b_mod, g_pre, g_post, w_in, w_out, na_rpb,
              diff_lq1, diff_lk1, diff_lq2, diff_lk2, diff_subln_g,
              conv_w, conv_b, conv_ln_g, conv_ln_b):
    xc = ctx
    for l in range(DEPTH):
        x, xc = _layer(x, xc, c, c_ctx, w_mod[l], b_mod[l], g_pre[l], g_post[l], w_in[l], w_out[l],
                       na_rpb[l], diff_lq1[l], diff_lk1[l], diff_lq2[l], diff_lk2[l], diff_subln_g[l],
                       conv_w[l], conv_b[l], conv_ln_g[l], conv_ln_b[l],
                       layer_idx=l, update_ctx=(l < DEPTH - 1))
    return x
```

```python
import bisect
import numpy as np
import concourse.bass as bass
import concourse.mybir as mybir
from concourse.bass_utils import run_bass_kernel_spmd

F32 = mybir.dt.float32
BF16 = mybir.dt.bfloat16
AF = mybir.ActivationFunctionType
ALU = mybir.AluOpType
AX = mybir.AxisListType


class Sched:
    NDMA_SEMS = 8

    def __init__(self, nc, self_sync=True):
        self.nc = nc
        self.self_sync = self_sync
        self.handles = {'pe': nc.tensor, 'act': nc.scalar, 'dve': nc.vector,
                        'pool': nc.gpsimd, 'sp': nc.sync}
        self._ctx = []

    def begin(self):
        nc = self.nc
        self.sem = {}
        self.cnt = {}
        self.seen = {}
        self.hist = {}
        self.semobj = {}
        for e in self.handles:
            cm = nc.semaphore("sem_" + e)
            s = cm.__enter__()
            self._ctx.append(cm)
            self.sem[e] = "E_" + e
            self.semobj["E_" + e] = s
            self.cnt[e] = 0
            self.seen[e] = {}
            self.hist["E_" + e] = ([], [])
        self.dsem = {}
        self.dcnt = {}
        self.dnext = {}
        for q in ('sp', 'pool', 'act'):
            names = []
            for i in range(self.NDMA_SEMS):
                cm = nc.semaphore("dsem_%s_%d" % (q, i))
                s = cm.__enter__()
                self._ctx.append(cm)
                nm = "D_%s_%d" % (q, i)
                self.semobj[nm] = s
                self.hist[nm] = ([], [])
                self.dcnt[nm] = 0
                names.append(nm)
            self.dsem[q] = names
            self.dnext[q] = 0
        self.lastw = {}
        self.readers = {}

    def _closure(self, tok):
        nm, v = tok
        vals, snaps = self.hist[nm]
        i = bisect.bisect_left(vals, v)
        if i < len(vals):
            return snaps[i]
        return {}

    def _wait(self, e, toks):
        need = {}
        for tok in toks:
            if tok is None:
                continue
            nm, v = tok
            if need.get(nm, 0) < v:
                need[nm] = v
        seen = self.seen[e]
        own = self.sem[e]
        items = [(nm, v) for nm, v in need.items() if seen.get(nm, 0) < v]
        items.sort(key=lambda t: -len(self._closure(t)))
        h = self.handles[e]
        for nm, v in items:
            if seen.get(nm, 0) >= v:
                continue
            h.wait_ge(self.semobj[nm], v)
            seen[nm] = v
            for k2, v2 in self._closure((nm, v)).items():
                if seen.get(k2, 0) < v2:
                    seen[k2] = v2

    def _deps(self, e, reads, writes):
        own = self.sem[e]
        toks = []
        for k in reads:
            t = self.lastw.get(k)
            if t is not None:
                if not (t[0] == own and not self.self_sync):
                    toks.append(t)
            if isinstance(k, tuple) and isinstance(k[0], str) and k[0].startswith('ps'):
                r = self.readers.get(k)
                if r:
                    for t2 in r.values():
                        if t2[0] != own:
                            toks.append(t2)
        for k in writes:
            t = self.lastw.get(k)
            if t is not None and t[0] != own:
                toks.append(t)
            r = self.readers.get(k)
            if r:
                for t2 in r.values():
                    if t2[0] != own:
                        toks.append(t2)
        return toks

    def _record(self, e, tok, reads, writes):
        for k in writes:
            self.lastw[k] = tok
            self.readers[k] = {}
        for k in reads:
            self.readers.setdefault(k, {})[e] = tok

    def op(self, e, fn, reads=(), writes=(), inc=True, extra=()):
        toks = self._deps(e, reads, writes)
        toks.extend(extra)
        self._wait(e, toks)
        ins = fn()
        nm = self.sem[e]
        tok = (nm, self.cnt[e] + 1)
        if inc:
            ins.then_inc(self.semobj[nm], 1)
            self.cnt[e] += 1
            vals, snaps = self.hist[nm]
            vals.append(self.cnt[e])
            snaps.append(dict(self.seen[e]))
        self._record(e, tok, reads, writes)
        return tok

    def dma(self, q, out, in_, reads=(), writes=(), extra=()):
        toks = self._deps(q, reads, writes)
        toks.extend(extra)
        i = self.dnext[q]
        self.dnext[q] = (i + 1) % self.NDMA_SEMS
        nm = self.dsem[q][i]
        if self.dcnt[nm] > 0:
            toks.append((nm, 16 * self.dcnt[nm]))
        self._wait(q, toks)
        h = self.handles[q]
        ins = h.dma_start(out=out, in_=in_)
        ins.then_inc(self.semobj[nm], 16)
        self.dcnt[nm] += 1
        tok = (nm, 16 * self.dcnt[nm])
        vals, snaps = self.hist[nm]
        vals.append(tok[1])
        snaps.append(dict(self.seen[q]))
        self._record(q, tok, reads, writes)
        return tok

    def barrier(self):
        toks = []
        for e in self.handles:
            if self.cnt[e] > 0:
                toks.append((self.sem[e], self.cnt[e]))
        for nm, c in self.dcnt.items():
            if c > 0:
                toks.append((nm, 16 * c))
        for e in self.handles:
            self._wait(e, list(toks))
        self.lastw = {}
        self.readers = {}

    def finish(self):
        self.barrier()
        for cm in reversed(self._ctx):
            cm.__exit__(None, None, None)
        self._ctx = []


D = 1024
T = 2048
CTX = 256
NTOK = T + CTX
NT = NTOK // 128
DEPTH = 4
EPS = 1e-6
NEG = -30000.0
import os
S5CUT = int(os.environ.get('S5CUT', '9'))
LAGR = 1
LAGD = 4
C_NAK, C_NAV, C_DFK, C_DFV = 0, 384, 768, 1152
C_NAQ, C_NAG, C_DFQ, C_DFG = 1536, 1920, 2304, 2688
C_CVA, C_CVB, C_CVG = 3072, 3328, 3584
NBLK = 24
UPAD = 15
U_LAT0 = UPAD
U_CTX0 = UPAD + T + 2 * UPAD
U_LEN = U_CTX0 + CTX + UPAD


def _na_rows(j):
    lo = max(4, 2 * j - 4)
    hi = min(27, 2 * j + 5)
    rows = list(range(lo, hi + 1))
    if j <= 3:
        rows = [0, 1, 2, 3] + rows
    if j >= 12:
        rows = rows + [28, 29, 30, 31]
    return rows


def _na_block(j, r):
    e = 2 * j - r
    if r <= 3 or r >= 28:
        b = 6 - e
        assert 0 <= b < 14
        return b
    b = 14 + (4 - e)
    assert 14 <= b < 24
    return b


def build_nc(n_layers=DEPTH, debug=False, stop_after=None):
    nc = bass.Bass("TRN2", target_bir_lowering=False)

    def din(name, shape, dt=F32):
        return nc.dram_tensor(name, list(shape), dt, kind="ExternalInput").ap()

    x_in = din("x", [T, D])
    ctx_in = din("ctx", [CTX, D])
    cT_in = din("cT", [128, 8])
    cctxT_in = din("cctxT", [128, 8])
    w_mod = din("w_mod", [DEPTH, D, 3 * D])
    b_mod = din("b_mod", [DEPTH, 3 * D])
    g_pre = din("g_pre", [DEPTH, D])
    g_post = din("g_post", [DEPTH, D])
    w_in = din("w_in", [DEPTH, D, 3840])
    w_out = din("w_out", [DEPTH, D, D])
    nab_val = din("nab_val", [DEPTH, 6, 128, NBLK * 64])
    nab_mask = din("nab_mask", [128, NBLK * 64])
    lqk = din("lqk", [DEPTH, 4, 48])
    subln_g = din("subln_g", [DEPTH, 96])
    conv_wT = din("conv_wT", [DEPTH, 256, 31])
    conv_b = din("conv_b", [DEPTH, 256])
    ln_g = din("ln_g", [DEPTH, 256])
    ln_b = din("ln_b", [DEPTH, 256])
    ident_in = din("ident", [128, 128])
    selc_in = din("selc", [2, 2, 128])
    ropec_in = din("ropec", [128, T])
    ropes_in = din("ropes", [128, T])
    perm_in = din("perm", [128, 128])
    out = nc.dram_tensor("out", [T, D], F32, kind="ExternalOutput").ap()
    Xs = nc.dram_tensor("Xs", [NTOK, D], F32, kind="Internal").ap()
    rows_d = nc.dram_tensor("rows_d", [DEPTH, 2, 3 * D], F32, kind="Internal").ap()
    wbp = [nc.dram_tensor("wbp%d" % l, [D, 1024], BF16, kind="Internal").ap() for l in range(DEPTH)]
    zpad_in = din("zpad", [1024, 64])
    wb_in = [nc.dram_tensor("wb_in%d" % l, [D, 3840], BF16, kind="Internal").ap() for l in range(DEPTH)]
    wb_out = [nc.dram_tensor("wb_out%d" % l, [D, D], BF16, kind="Internal").ap() for l in range(DEPTH)]
    if debug:
        dbg_yin = nc.dram_tensor("dbg_yin", [n_layers, NTOK, D], BF16, kind="ExternalOutput").ap()
        dbg_x = nc.dram_tensor("dbg_x", [n_layers, NTOK, D], F32, kind="ExternalOutput").ap()

    S = Sched(nc)
    op = S.op

    def wview(ap2d):
        return ap2d.rearrange("(kc p) n -> p kc n", p=128)

    import contextlib
    es = contextlib.ExitStack()

    def sb(name, shape, dt):
        return es.enter_context(nc.sbuf_tensor("s_" + name, list(shape), dt))

    with es:
        PP = [es.enter_context(nc.psum_tensor("pp%d" % i, [128, 1024], F32)) for i in range(4)]
        psT = [PP[0][:, i * 512:(i + 1) * 512].bitcast(BF16) for i in range(2)]
        psf = [PP[1 + i // 2][:, (i % 2) * 512:(i % 2 + 1) * 512] for i in range(6)]
        PT = [('psT', i) for i in range(2)]
        PF = [('psf', i) for i in range(6)]

        identf = sb("identf", [128, 128], F32)
        identb = sb("identb", [128, 128], BF16)
        permf = sb("permf", [128, 128], F32)
        permb = sb("permb", [128, 128], BF16)
        selc = sb("selc", [2, 2, 128], F32)
        cT = sb("cTs", [128, 2, 8], F32)
        sc2 = sb("sc2", [128, 8, 2], F32)
        hT = sb("hT", [128, 8, NTOK], BF16)
        yin = sb("yin", [128, NT, D], BF16)
        tabG = sb("tabG", [128, 2, D], F32)
        tabAB = sb("tabAB", [128, 2, 2, D], F32)
        junk = sb("junk", [128, D], BF16)
        W3 = sb("W3", [128, 8, 1408], BF16)
        nhalf = sb("nhalf", [128, 32], F32)

        S.begin()
        op_ = S.op
        op_('pool', lambda: nc.gpsimd.memset(nhalf[:], -0.5), [], ['nhalf'])

        cast_q = {l_: [] for l_ in range(n_layers)}
        CGROUPS = [(0, 768), (1536, 384), (1152, 384), None, (3072, 512),
                   (1920, 384), (2688, 384), (3584, 256)]
        for l in range(n_layers):
            for gi_, cg_ in enumerate(CGROUPS):
                for rh in range(2):
                    rs = slice(rh * 512, (rh + 1) * 512)
                    if cg_ is not None:
                        c0_, n_ = cg_
                        cast_q[l].append((wb_in[l][rs, c0_:c0_ + n_], w_in[l, rs, c0_:c0_ + n_], ('wbi', l, gi_, rh)))
                    elif rh == 0:
                        for off_, c0_ in ((0, C_DFK), (512, C_DFQ)):
                            for m_ in range(8):
                                cast_q[l].append((wbp[l][:, off_ + m_ * 64:off_ + m_ * 64 + 48],
                                                  w_in[l, :, c0_ + m_ * 48:c0_ + (m_ + 1) * 48],
                                                  ('wbp', l, off_, m_)))
                                cast_q[l].append((wbp[l][:, off_ + m_ * 64 + 48:off_ + (m_ + 1) * 64],
                                                  zpad_in[:, 0:16], ('wbpz', l, off_, m_)))
            for rh in range(2):
                rs = slice(rh * 512, (rh + 1) * 512)
                cast_q[l].append((wb_out[l][rs, :], w_out[l, rs, :], ('wbo', l, rh)))

        def pump_casts(l_, k=1):
            if l_ >= n_layers:
                return
            for _ in range(k):
                if cast_q[l_]:
                    o_, i_, key_ = cast_q[l_].pop(0)
                    S.dma('pool', o_, i_, writes=[key_])
        pump_casts(0, 4)

        def WBI(l, *gis):
            ks = []
            for g_ in gis:
                if g_ == 3:
                    for off_ in (0, 512):
                        for m_ in range(8):
                            ks += [('wbp', l, off_, m_), ('wbpz', l, off_, m_)]
                else:
                    ks += [('wbi', l, g_, rh) for rh in range(2)]
            return ks
        def WBO(l): return [('wbo', l, rh) for rh in range(2)]

        S.dma('sp', identf[:], ident_in[:, :], writes=['identf'])
        S.dma('sp', permf[:], perm_in[:, :], writes=['permf'])
        S.dma('sp', selc[:], selc_in[:, :, :], writes=['selc'])
        S.dma('sp', cT[:, 0, :], cT_in[:, :], writes=['cT'])
        S.dma('sp', cT[:, 1, :], cctxT_in[:, :], writes=['cT'])
        op('dve', lambda: nc.vector.tensor_copy(identb[:], identf[:]), ['identf'], ['identb'])
        op('dve', lambda: nc.vector.tensor_copy(permb[:], permf[:]), ['permf'], ['permb'])
        op('act', lambda: nc.scalar.activation(sc2[:, :, 0], cT[:, 0, :], AF.Silu), ['cT'], ['sc2'])
        op('act', lambda: nc.scalar.activation(sc2[:, :, 1], cT[:, 1, :], AF.Silu), ['cT'], ['sc2'])

        def hkeys(tok0, n):
            return [('hT', t) for t in range(tok0 // 128, (tok0 + n + 127) // 128)]

        cpy_rr = [0]

        def evac_copy(dst, src, reads, writes, scale=None):
            cpy_rr[0] ^= 1
            if cpy_rr[0]:
                if scale is None:
                    op('act', lambda: nc.scalar.copy(dst, src), reads, writes)
                else:
                    op('act', lambda: nc.scalar.activation(dst, src, AF.Copy, scale=scale), reads, writes)
            else:
                if scale is None:
                    op('dve', lambda: nc.vector.tensor_copy(dst, src), reads, writes)
                else:
                    op('dve', lambda: nc.vector.tensor_scalar(dst, src, scale, None, ALU.mult), reads, writes)

        for l in range(n_layers):
            last = (l == DEPTH - 1)
            upd = not last
            lam_init = 0.8 - 0.6 * float(np.exp(-0.3 * l))
            ntile_q = NT if upd else 16
            Xsrc = (lambda t: (x_in[t * 128:(t + 1) * 128, :] if t < 16 else ctx_in[(t - 16) * 128:(t - 15) * 128, :])) \
                if l == 0 else (lambda t: Xs[t * 128:(t + 1) * 128, :])
            xreads = (lambda t: []) if l == 0 else (lambda t: [('Xs', t)])

            def s0a_load(l2, n, wm, bmp, gpp, rowp, bi):
                S.dma('sp', wm[bi][:], wview(w_mod[l2])[:, :, n * 512:(n + 1) * 512], writes=[('wm', bi)])

            def s0a_compute(l2, n, wm, bmp, gpp, rowp, bi):
                ti_src, half = n // 2, n % 2
                for p in range(2):
                    S.dma('sp', bmp[0][p:p + 1, :], b_mod[l2:l2 + 1, n * 512:(n + 1) * 512], writes=[('bmp', 0)])
                    if ti_src == 1:
                        S.dma('sp', gpp[0][p:p + 1, :], g_pre[l2:l2 + 1, half * 512:(half + 1) * 512], writes=[('gpp', 0)])
                    elif ti_src == 2:
                        S.dma('sp', gpp[0][p:p + 1, :], g_post[l2:l2 + 1, half * 512:(half + 1) * 512], writes=[('gpp', 0)])
                w = wm[bi]
                ps, psk = psf[4 + bi], PF[4 + bi]
                for kc in range(8):
                    op('pe', lambda: nc.tensor.matmul(ps[0:2, :], lhsT=sc2[:, kc, :], rhs=w[:, kc, :],
                                                      start=(kc == 0), stop=(kc == 7)),
                       ['sc2', ('wm', bi)], [psk], inc=(kc == 7))
                rp = rowp[bi]
                op('dve', lambda: nc.vector.tensor_tensor(rp[0:2, :], ps[0:2, :], bmp[bi][0:2, :], ALU.add),
                   [psk, ('bmp', 0)], [('rowp', 0)])
                if ti_src == 1:
                    op('dve', lambda: nc.vector.scalar_tensor_tensor(rp[0:2, :], rp[0:2, :], 1.0, gpp[bi][0:2, :],
                                                                     ALU.add, ALU.mult), [('rowp', 0), ('gpp', 0)], [('rowp', 0)])
                elif ti_src == 2:
                    op('dve', lambda: nc.vector.tensor_tensor(rp[0:2, :], rp[0:2, :], gpp[bi][0:2, :], ALU.mult),
                       [('rowp', 0), ('gpp', 0)], [('rowp', 0)])
                dst = {0: 1, 1: 0, 2: 2}[ti_src]
                S.dma('sp', rows_d[l2, :, dst * D + half * 512:dst * D + (half + 1) * 512], rp[0:2, :],
                      reads=[('rowp', 0)], writes=[('rows_d', l2, n)])

            def s0a_piece(l2, n, wm, bmp, gpp, rowp, bi):
                s0a_load(l2, n, wm, bmp, gpp, rowp, bi)
                s0a_compute(l2, n, wm, bmp, gpp, rowp, bi)

            def s0a_bufs(st_, tag):
                def sbx(name, shape, dt):
                    return st_.enter_context(nc.sbuf_tensor("s_%s_%s_%d" % (name, tag, l), list(shape), dt))
                wm = [sbx("wm%d" % i, [128, 8, 512], F32) for i in range(2)]
                bmp = [sbx("bmp0", [2, 512], F32)] * 2
                gpp = [sbx("gpp0", [2, 512], F32)] * 2
                rowp = [sbx("rowp0", [2, 512], F32)] * 2
                return wm, bmp, gpp, rowp

            def s0b(l2, tis, outer=None):
                with contextlib.ExitStack() as st_:
                    t0_ = tis[0]
                    nt_ = len(tis)
                    rows2 = (outer or st_).enter_context(
                        nc.sbuf_tensor("s_rows2_%d_%d_%d" % (l, l2, t0_), [2, nt_ * D], F32))
                    S.dma('sp', rows2[:], rows_d[l2, :, t0_ * D:(t0_ + nt_) * D],
                          reads=[('rows_d', l2, n) for n in range(6)], writes=['rows2'])
                    k = 0
                    for ti in tis:
                        for which in range(2):
                            for half in range(2):
                                ps = psf[2 + (k % 2)]
                                c0_ = (ti - t0_) * D + half * 512
                                op('pe', lambda: nc.tensor.matmul(ps[:, :], lhsT=selc[0:2, which, :],
                                                                  rhs=rows2[0:2, c0_:c0_ + 512],
                                                                  start=True, stop=True),
                                   ['selc', 'rows2'], [PF[2 + (k % 2)]])
                                dstt = tabG[:, which, :] if ti == 2 else tabAB[:, ti, which, :]
                                evac_copy(dstt[:, half * 512:(half + 1) * 512], ps[:, :],
                                          [PF[2 + (k % 2)]], [('tabs', ti, which)])
                                k += 1
                    if outer is None:
                        S.barrier()

            def prenorm_front(t, xsrc_ap, xkey, tmpf_, hb_, stat_, use_pool=True):
                which = 0 if t < 16 else 1
                op('act', lambda: nc.scalar.activation(junk[:], xsrc_ap, AF.Square, accum_out=stat_[:, 0, t:t + 1]),
                   [xkey], ['junk', ('stat', t)])
                op('dve', lambda: nc.vector.tensor_scalar(stat_[:, 1, t:t + 1], stat_[:, 0, t:t + 1], 1.0 / D, EPS,
                                                          ALU.mult, ALU.add), [('stat', t)], [('stat1', t)])
                if use_pool:
                    op('pool', lambda: nc.gpsimd.tensor_tensor(stat_[:, 3, t:t + 1], stat_[:, 1, t:t + 1], nhalf[:, 0:1], ALU.pow),
                       [('stat1', t), 'nhalf'], [('stat3', t)])
                else:
                    op('act', lambda: nc.scalar.activation(stat_[:, 2, t:t + 1], stat_[:, 1, t:t + 1], AF.Sqrt),
                       [('stat1', t)], [('stat2', t)])
                    op('dve', lambda: nc.vector.reciprocal(stat_[:, 3, t:t + 1], stat_[:, 2, t:t + 1]),
                       [('stat2', t)], [('stat3', t)])
                op('dve', lambda: nc.vector.scalar_tensor_tensor(tmpf_[0][:], xsrc_ap, stat_[:, 3, t:t + 1],
                                                                 tabAB[:, 0, which, :], ALU.mult, ALU.mult),
                   [xkey, ('stat3', t), ('tabs', 0, which)], [tmpf_[1]])
                if use_pool:
                    op('pool', lambda: nc.gpsimd.tensor_tensor(hb_[0][:], tmpf_[0][:], tabAB[:, 1, which, :], ALU.add),
                       [tmpf_[1], ('tabs', 1, which)], [hb_[1]])
                else:
                    op('dve', lambda: nc.vector.tensor_tensor(hb_[0][:], tmpf_[0][:], tabAB[:, 1, which, :], ALU.add),
                       [tmpf_[1], ('tabs', 1, which)], [hb_[1]])

            def prenorm_back(t, hb_, pb_):
                for kc in range(8):
                    op('pe', lambda: nc.tensor.transpose(psT[pb_][:, kc * 128:(kc + 1) * 128],
                                                         hb_[0][:, kc * 128:(kc + 1) * 128], identb[:]),
                       [hb_[1], 'identb'], [PT[pb_]], inc=(kc == 7))
                evac_copy(hT[:, :, t * 128:(t + 1) * 128], psT[pb_][:].rearrange("p (a b) -> p a b", b=128),
                          [PT[pb_]], [('hT', t)])

            def prenorm_tile(t, xsrc_ap, xkey, tmpf_, hb_, stat_, pb_):
                prenorm_front(t, xsrc_ap, xkey, tmpf_, hb_, stat_, use_pool=False)
                prenorm_back(t, hb_, pb_)

            if l == 0:
                with contextlib.ExitStack() as st:
                    bufs0 = s0a_bufs(st, "a")
                    s0a_load(0, 0, *bufs0, 0)
                    for n in range(4):
                        if n + 1 < 4:
                            s0a_load(0, n + 1, *bufs0, (n + 1) % 2)
                        s0a_compute(0, n, *bufs0, n % 2)
                S.barrier()
                s0b(0, [0, 1])
                with contextlib.ExitStack() as st:
                    def sb2(name, shape, dt):
                        return st.enter_context(nc.sbuf_tensor("s_%s_%d" % (name, l), list(shape), dt))
                    NB1 = 4
                    xt = [sb2("xt%d" % i, [128, D], F32) for i in range(NB1)]
                    tmpf = [sb2("tmpf%d" % i, [128, D], F32) for i in range(NB1)]
                    hb = [sb2("hb%d" % i, [128, D], BF16) for i in range(NB1)]
                    stat = sb2("stat", [128, 4, NT], F32)
                    for t in range(NT):
                        b = t % NB1
                        S.dma('sp', xt[b][:], Xsrc(t), reads=xreads(t), writes=[('xt', b)])
                        prenorm_tile(t, xt[b][:], ('xt', b), (tmpf[b], ('tmpf', b)), (hb[b], ('hb', b)), stat, t % 2)
                S.barrier()
            if stop_after in ('S0', 'S1'):
                break

            def mm_fm(ps, pskey, pbase, M, w, wkey, c0, tok0, N, inc_last=True):
                for kc in range(8):
                    op('pe', lambda: nc.tensor.matmul(ps[pbase:pbase + M, 0:N], lhsT=w[:, kc, c0:c0 + M],
                                                      rhs=hT[:, kc, tok0:tok0 + N], start=(kc == 0), stop=(kc == 7)),
                       (wkey if isinstance(wkey, list) else [wkey]) + hkeys(tok0, N), [pskey], inc=(inc_last and kc == 7))

            def mm_tm(ps, pskey, t, w, wkey, c0, N):
                for kc in range(8):
                    op('pe', lambda: nc.tensor.matmul(ps[:, 0:N], lhsT=hT[:, kc, t * 128:(t + 1) * 128],
                                                      rhs=w[:, kc, c0:c0 + N], start=(kc == 0), stop=(kc == 7)),
                       (wkey if isinstance(wkey, list) else [wkey]) + [('hT', t)], [pskey], inc=(kc == 7))

            groups = [(g * 512, 512) for g in range(4)] + [(T, CTX)]
            W3K = [('W3', 0), ('W3', 1), ('W3', 2), ('W3', 3)]

            def load_kqv(l2, ck, cq, cv, gk, gq, gv):
                S.dma('sp', W3[:, :, 0:384], wview(wb_in[l2])[:, :, ck:ck + 384], reads=WBI(l2, gk), writes=[W3K[0]])
                S.dma('sp', W3[:, :, 384:768], wview(wb_in[l2])[:, :, cq:cq + 384], reads=WBI(l2, gq), writes=[W3K[1]])
                S.dma('sp', W3[:, :, 768:1152], wview(wb_in[l2])[:, :, cv:cv + 384], reads=WBI(l2, gv), writes=[W3K[2]])

            with contextlib.ExitStack() as st:
                def sb2(name, shape, dt):
                    return st.enter_context(nc.sbuf_tensor("s_%s_%d" % (name, l), list(shape), dt))
                wk, wq, wv = W3[:, :, 0:384], W3[:, :, 384:768], W3[:, :, 768:1152]
                kT = sb2("kT", [128, 3, NTOK], BF16)
                qT = sb2("qT", [128, 3, NTOK], BF16)
                Vn = sb2("Vn", [128, NT, 6, 65], BF16)
                btab = sb2("btab", [128, 6, NBLK * 64], BF16)
                bval = [sb2("bval%d" % i, [128, NBLK * 64], F32) for i in range(1)]
                maskt = sb2("maskt", [128, NBLK * 64], F32)
                scf2 = [sb2("scf%d" % i, [128, 2, 512], F32) for i in range(2)]
                pT2 = [sb2("pT%d" % i, [128, 2, 512], BF16) for i in range(3)]
                rc = [sb2("rc%d" % i, [128, 4], F32) for i in range(4)]

                if l == 0:
                    load_kqv(l, C_NAK, C_NAQ, C_NAV, 0, 1, 0)
                S.dma('sp', maskt[:], nab_mask[:, :], writes=['maskt'])
                op('pool', lambda: nc.gpsimd.memset(Vn[:], 1.0), [], ['Vn'])
                for h in range(6):
                    S.dma('sp', bval[0][:], nab_val[l, h, :, :], writes=[('bval', 0)])
                    op('pool', lambda: nc.gpsimd.tensor_tensor(btab[:, h, :], bval[0][:], maskt[:], ALU.add),
                       [('bval', 0), 'maskt'], [('btab', h)])
                if l == 0:
                    pump_casts(0, 100)
                it = 0
                for c in range(3):
                    for gi, (tok0, N) in enumerate(groups):
                        ps = psf[it % 2]
                        mm_fm(ps, PF[it % 2], 0, 128, wk, W3K[0], c * 128, tok0, N)
                        evac_copy(kT[:, c, tok0:tok0 + N], ps[:, 0:N], [PF[it % 2]], [('kT', c, gi)])
                        it += 1
                        if gi == 4 and not upd:
                            continue
                        ps = psf[it % 2]
                        mm_fm(ps, PF[it % 2], 0, 128, wq, W3K[1], c * 128, tok0, N)
                        evac_copy(qT[:, c, tok0:tok0 + N], ps[:, 0:N], [PF[it % 2]], [('qT', c, gi)], scale=0.125)
                        it += 1
                for t in range(NT):
                    ps = psf[it % 2]
                    mm_tm(ps, PF[it % 2], t, wv, W3K[2], 0, 384)
                    evac_copy(Vn[:, t, :, 0:64], ps[:, 0:384].rearrange("p (h d) -> p h d", d=64),
                              [PF[it % 2], 'Vn'], [('Vn', t)])
                    it += 1
                S.dma('sp', W3[:, :, 0:512], wview(wbp[l])[:, :, 0:512], reads=WBI(l, 3), writes=[W3K[0], W3K[1]])
                S.dma('sp', W3[:, :, 512:1024], wview(wbp[l])[:, :, 512:1024], reads=WBI(l, 3), writes=[W3K[1], W3K[2]])
                S.dma('sp', W3[:, :, 1024:1408], wview(wb_in[l])[:, :, C_DFV:C_DFV + 384], reads=WBI(l, 2), writes=[W3K[2], W3K[3]])
                Sb = [psf[0], psf[1], psT[0][:].bitcast(F32), psT[1][:].bitcast(F32)]
                Sk = [PF[0], PF[1], PT[0], PT[1]]
                units = []
                for c in range(3):
                    for G in range(5 if upd else 4):
                        if G < 4:
                            qt0, nq = 4 * G, 4
                            plan = []
                            for j in range(16):
                                rws = [r for r in _na_rows(j) if 8 * G <= r < 8 * G + 8]
                                if rws:
                                    assert rws[0] % 2 == 0 and len(rws) % 2 == 0
                                    plan.append((j, rws))
                            plan += [(16, None), (17, None)]
                        else:
                            qt0, nq = 16, 2
                            plan = [(16, None), (17, None)]
                        gid = c * 5 + G
                        for pi, (j, rws) in enumerate(plan):
                            units.append(dict(c=c, G=G, qt0=qt0, nq=nq, j=j, rws=rws, gid=gid,
                                              first=(pi == 0), last=(pi == len(plan) - 1)))
                DLAG = 2

                def na_front(ui):
                    u = units[ui]
                    c, j, rws = u['c'], u['j'], u['rws']
                    if rws is None:
                        i0, n_i = u['qt0'], u['nq']
                    else:
                        i0, n_i = rws[0] // 2, len(rws) // 2
                    N = n_i * 128
                    u['i0'], u['n_i'], u['N'] = i0, n_i, N
                    gq = min(i0 * 128 // 512, 4)
                    for hp in range(2):
                        pb = hp * 64
                        ps, psk = Sb[2 * (ui % 2) + hp], Sk[2 * (ui % 2) + hp]
                        op('pe', lambda: nc.tensor.matmul(ps[:, 0:N], lhsT=kT[pb:pb + 64, c, j * 128:(j + 1) * 128],
                                                          rhs=qT[pb:pb + 64, c, i0 * 128:i0 * 128 + N],
                                                          start=True, stop=True),
                           [('kT', c, min(j // 4, 4)), ('qT', c, gq)], [psk])
                    pk = ('pT', ui % 3)
                    pout = pT2[ui % 3][:, :, 0:N]
                    pskeys = [Sk[2 * (ui % 2)], Sk[2 * (ui % 2) + 1]]
                    if rws is not None:
                        sfk = ('scf', ui % 2)
                        for hp in range(2):
                            h = 2 * c + hp
                            ps, psk = Sb[2 * (ui % 2) + hp], Sk[2 * (ui % 2) + hp]
                            sf = scf2[ui % 2][:, hp, :]
                            segs = []
                            for r in rws:
                                bb = _na_block(j, r)
                                if segs and segs[-1][1] + segs[-1][2] == bb:
                                    segs[-1][2] += 1
                                else:
                                    segs.append([r, bb, 1])
                            for (r0, b0, nb) in segs:
                                o0 = (r0 - rws[0]) * 64
                                op('dve', lambda: nc.vector.tensor_tensor(sf[:, o0:o0 + nb * 64], ps[:, o0:o0 + nb * 64],
                                                                          btab[:, h, b0 * 64:(b0 + nb) * 64], ALU.add),
                                   [psk, ('btab', h)], [sfk])
                        op('act', lambda: nc.scalar.activation(pout, scf2[ui % 2][:, :, 0:N], AF.Exp), [sfk], [pk])
                    else:
                        pin = PP[1 - (ui % 2)][:].rearrange("p (m n) -> p m n", m=2)[:, :, 0:N]
                        op('act', lambda: nc.scalar.activation(pout, pin, AF.Exp), pskeys, [pk])

                def na_back(ui):
                    u = units[ui]
                    c, j, G, qt0, nq = u['c'], u['j'], u['G'], u['qt0'], u['nq']
                    i0, n_i = u['i0'], u['n_i']
                    for hp in range(2):
                        h = 2 * c + hp
                        p_t, pk = pT2[ui % 3][:, hp, :], ('pT', ui % 3)
                        ai = 2 + 2 * (u['gid'] % 2) + hp
                        acc, acck = psf[ai], PF[ai]
                        for ii in range(n_i):
                            slot = i0 + ii - qt0
                            op('pe', lambda: nc.tensor.matmul(acc[:, slot * 65:(slot + 1) * 65],
                                                              lhsT=p_t[:, ii * 128:(ii + 1) * 128],
                                                              rhs=Vn[:, j, h, :], start=(u['first'] and ii == 0), stop=True,
                                                              skip_group_check=True),
                               [pk, 'Vn', ('Vn', j)], [acck])
                    if u['last']:
                        for hp in range(2):
                            h = 2 * c + hp
                            ai = 2 + 2 * (u['gid'] % 2) + hp
                            acc, acck = psf[ai], PF[ai]
                            accv = acc[:, 0:nq * 65].rearrange("p (i d) -> p i d", d=65)
                            r_ = rc[ai - 2]
                            op('dve', lambda: nc.vector.reciprocal(r_[:, 0:nq], accv[:, :, 64]), [acck], [('rc', ai)])
                            op('dve', lambda: nc.vector.tensor_tensor(
                                yin[:, qt0:qt0 + nq, h * 64:(h + 1) * 64], accv[:, :, 0:64],
                                r_[:, 0:nq].unsqueeze(2).broadcast_to([128, nq, 64]), ALU.mult),
                               [acck, ('rc', ai)], [('yin', 0, G)])

                for s in range(len(units) + DLAG):
                    if s < len(units):
                        na_front(s)
                    if s - DLAG >= 0:
                        na_back(s - DLAG)
            S.barrier()

            if stop_after == 'S2':
                break
            with contextlib.ExitStack() as st:
                def sb2(name, shape, dt):
                    return st.enter_context(nc.sbuf_tensor("s_%s_%d" % (name, l), list(shape), dt))
                wk, wq, wv = W3[:, :, 0:512], W3[:, :, 512:1024], W3[:, :, 1024:1408]
                kT = sb2("dkT", [128, 4, NTOK], BF16)
                qT = sb2("dqT", [128, 4, NTOK], BF16)
                Vd = sb2("Vd", [128, NT, 4, 97], BF16)
                rc_ = sb2("ropec", [128, T], F32)
                rs_ = sb2("ropes", [128, T], F32)
                xraw = [sb2("xraw%d" % i, [128, 512], BF16) for i in range(2)]
                rt1 = [sb2("rt1_%d" % i, [128, 512], F32) for i in range(1)]
                rt2 = [sb2("rt2_%d" % i, [128, 512], F32) for i in range(1)]
                pT2 = [sb2("dpT%d" % i, [128, 2, 512], BF16) for i in range(2)]
                lq = sb2("lq", [128, 4, 48], F32)
                lsm = sb2("lsm", [128, 16], F32)
                gsub = sb2("gsub", [128, 96], F32)
                of1 = [sb2("of1_%d" % i, [128, 4, 96], F32) for i in range(2)]
                of2 = [sb2("of2_%d" % i, [128, 4, 96], F32) for i in range(2)]
                sm = [sb2("dsm%d" % i, [128, 8, 4], F32) for i in range(2)]

                S.dma('sp', rc_[:], ropec_in[:, :], writes=['ropec'])
                S.dma('sp', rs_[:], ropes_in[:, :], writes=['ropes'])
                for i in range(4):
                    S.dma('sp', lq[:, i, :], lqk[l, i, :].partition_broadcast(128), writes=['lq'])
                S.dma('sp', gsub[:], subln_g[l, :].partition_broadcast(128), writes=['gsub0'])
                op('pool', lambda: nc.gpsimd.memset(Vd[:], 1.0), [], ['Vd'])
                for i in range(2):
                    op('pool', lambda: nc.gpsimd.memset(xraw[i][:], 0.0), [], [('xraw', i)])
                op('dve', lambda: nc.vector.tensor_tensor(lq[:, 0, :], lq[:, 0, :], lq[:, 1, :], ALU.mult), ['lq'], ['lq'])
                op('dve', lambda: nc.vector.tensor_tensor(lq[:, 2, :], lq[:, 2, :], lq[:, 3, :], ALU.mult), ['lq'], ['lq'])
                op('dve', lambda: nc.vector.tensor_reduce(lsm[:, 0:1], lq[:, 0, :], AX.X, ALU.add), ['lq'], ['lsm'])
                op('dve', lambda: nc.vector.tensor_reduce(lsm[:, 1:2], lq[:, 2, :], AX.X, ALU.add), ['lq'], ['lsm'])
                op('act', lambda: nc.scalar.activation(lsm[:, 2:4], lsm[:, 0:2], AF.Exp), ['lsm'], ['lsm2'])
                op('dve', lambda: nc.vector.tensor_tensor(lsm[:, 4:5], lsm[:, 3:4], lsm[:, 2:3], ALU.subtract),
                   ['lsm2'], ['lsm3'])
                op('dve', lambda: nc.vector.tensor_scalar(lsm[:, 5:6], lsm[:, 4:5], -lam_init, None, ALU.add),
                   ['lsm3'], ['nlam'])
                op('dve', lambda: nc.vector.tensor_scalar(gsub[:], gsub[:], 1.0 - lam_init, None, ALU.mult),
                   ['gsub0'], ['gsub'])
                nlam = lsm[:, 5:6]

                it = 0
                pend = []

                def rope_back(it_, dst, dkey, c, gi, tok0, N):
                    xr = xraw[it_ % 2]
                    xk = ('xraw', it_ % 2)
                    ps2 = psf[2 + (it_ % 2)]
                    ps2k = PF[2 + (it_ % 2)]
                    op('pe', lambda: nc.tensor.matmul(ps2[:, 0:N], lhsT=permb[:], rhs=xr[:, 0:N],
                                                      start=True, stop=True), ['permb', xk], [ps2k])
                    t1 = rt1[0]
                    t2 = rt2[0]
                    op('pool', lambda: nc.gpsimd.tensor_tensor(t1[:, 0:N], xr[:, 0:N], rc_[:, tok0:tok0 + N], ALU.mult),
                       [xk, 'ropec'], [('rt1', 0)])
                    op('dve', lambda: nc.vector.tensor_tensor(t2[:, 0:N], ps2[:, 0:N], rs_[:, tok0:tok0 + N], ALU.mult),
                       [ps2k, 'ropes'], [('rt2', 0)])
                    op('dve', lambda: nc.vector.tensor_tensor(dst[:, c, tok0:tok0 + N], t1[:, 0:N], t2[:, 0:N], ALU.add),
                       [('rt1', 0), ('rt2', 0)], [(dkey, c, gi)])

                for (w, wkey, dst, dkey, cbase) in ((wk, [W3K[0], W3K[1]], kT, 'kT', 0), (wq, [W3K[1], W3K[2]], qT, 'qT', 0)):
                    for c in range(4):
                        for gi, (tok0, N) in enumerate(groups):
                            if dkey == 'qT' and gi == 4 and not upd:
                                continue
                            ps = psf[it % 2]
                            psk = PF[it % 2]
                            mm_fm(ps, psk, 0, 128, w, wkey, c * 128, tok0, N)
                            if gi == 4:
                                evac_copy(dst[:, c, tok0:tok0 + N], ps[:, 0:N], [psk], [(dkey, c, gi)])
                            else:
                                xr = xraw[it % 2]
                                xk = ('xraw', it % 2)
                                evac_copy(xr[:, 0:N], ps[:, 0:N], [psk], [xk])
                                pend.append((it, dst, dkey, c, gi, tok0, N))
                            if LAGR == 0:
                                while pend:
                                    rope_back(*pend.pop(0))
                            elif len(pend) > 1 or (pend and pend[0][0] < it):
                                rope_back(*pend.pop(0))
                            it += 1
                while pend:
                    rope_back(*pend.pop(0))
                for t in range(NT):
                    ps = psf[it % 2]
                    mm_tm(ps, PF[it % 2], t, wv, [W3K[2], W3K[3]], 0, 384)
                    evac_copy(Vd[:, t, :, 0:96], ps[:, 0:384].rearrange("p (h d) -> p h d", d=96),
                              [PF[it % 2], 'Vd'], [('Vd', t)])
                    it += 1
                S.dma('sp', W3[:, :, 0:512], wview(wb_in[l])[:, :, C_CVA:C_CVA + 512], reads=WBI(l, 4),
                      writes=[W3K[0], W3K[1]])
                dscale = 48.0 ** -0.5
                Sb = [psf[0], psf[1], psT[0][:].bitcast(F32), psT[1][:].bitcast(F32)]
                Sk = [PF[0], PF[1], PT[0], PT[1]]
                units = []
                for hd in range(4):
                    for G in range(5 if upd else 4):
                        if G < 4:
                            qt0, nq, kts = 4 * G, 4, list(range(NT))
                        else:
                            qt0, nq, kts = 16, 2, [16, 17]
                        gid = hd * 5 + G
                        for ki, kt in enumerate(kts):
                            units.append(dict(hd=hd, G=G, qt0=qt0, nq=nq, kt=kt, gid=gid,
                                              first=(ki == 0), last=(ki == len(kts) - 1)))
                DLAG = 1

                def df_front(ui):
                    u = units[ui]
                    hd, G, qt0, nq, kt = u['hd'], u['G'], u['qt0'], u['nq'], u['kt']
                    N = nq * 128
                    for mm in range(2):
                        pb = mm * 64
                        ps, psk = Sb[2 * (ui % 2) + mm], Sk[2 * (ui % 2) + mm]
                        op('pe', lambda: nc.tensor.matmul(ps[:, 0:N], lhsT=kT[pb:pb + 48, hd, kt * 128:(kt + 1) * 128],
                                                          rhs=qT[pb:pb + 48, hd, qt0 * 128:qt0 * 128 + N],
                                                          start=True, stop=True),
                           [('kT', hd, min(kt // 4, 4)), ('qT', hd, G)], [psk])
                    pin = PP[1 - (ui % 2)][:].rearrange("p (m n) -> p m n", m=2)[:, :, 0:N]
                    op('act', lambda: nc.scalar.activation(pT2[ui % 2][:, :, 0:N], pin, AF.Exp, scale=dscale),
                       [Sk[2 * (ui % 2)], Sk[2 * (ui % 2) + 1]], [('pT', ui % 2)])

                def df_back(ui):
                    u = units[ui]
                    hd, G, qt0, nq, kt = u['hd'], u['G'], u['qt0'], u['nq'], u['kt']
                    gb = u['gid'] % 2
                    accs = [psf[2 + 2 * gb + mm] for mm in range(2)]
                    acck = [PF[2 + 2 * gb + mm] for mm in range(2)]
                    for mm in range(2):
                        for ii in range(nq):
                            op('pe', lambda: nc.tensor.matmul(accs[mm][:, ii * 97:(ii + 1) * 97],
                                                              lhsT=pT2[ui % 2][:, mm, ii * 128:(ii + 1) * 128],
                                                              rhs=Vd[:, kt, hd, :], start=(u['first'] and ii == 0), stop=True,
                                                              skip_group_check=True),
                               [('pT', ui % 2), 'Vd', ('Vd', kt)], [acck[mm]])
                    if not u['last']:
                        return
                    b = gb
                    a1 = accs[0][:, 0:nq * 97].rearrange("p (i d) -> p i d", d=97)
                    a2 = accs[1][:, 0:nq * 97].rearrange("p (i d) -> p i d", d=97)
                    s_ = sm[b]
                    o1 = of1[b][:, 0:nq, :]
                    o2 = of2[b][:, 0:nq, :]
                    K1, K2, KS, KO1, KO2 = acck[0], acck[1], ('dsm', b), ('of1', b), ('of2', b)

                    def bc(ap2):
                        return ap2.unsqueeze(2).broadcast_to([128, nq, 96])
                    op('dve', lambda: nc.vector.reciprocal(s_[:, 0, 0:nq], a1[:, :, 96]), [K1], [KS])
                    op('dve', lambda: nc.vector.reciprocal(s_[:, 1, 0:nq], a2[:, :, 96]), [K2], [KS])
                    op('dve', lambda: nc.vector.tensor_scalar(s_[:, 2, 0:nq], s_[:, 1, 0:nq], nlam, None, ALU.mult),
                       [KS, 'nlam'], [KS])
                    op('dve', lambda: nc.vector.tensor_tensor(o1, a1[:, :, 0:96], bc(s_[:, 0, 0:nq]), ALU.mult),
                       [K1, KS], [KO1])
                    op('dve', lambda: nc.vector.tensor_tensor(o2, a2[:, :, 0:96], bc(s_[:, 2, 0:nq]), ALU.mult),
                       [K2, KS], [KO2])
                    op('dve', lambda: nc.vector.tensor_tensor(o1, o1, o2, ALU.add), [KO1, KO2], [KO1])
                    op('dve', lambda: nc.vector.tensor_tensor(o2, o1, o1, ALU.mult), [KO1], [KO2])
                    op('dve', lambda: nc.vector.tensor_reduce(s_[:, 3, 0:nq], o2, AX.X, ALU.add), [KO2], [KS])
                    op('dve', lambda: nc.vector.tensor_scalar(s_[:, 4, 0:nq], s_[:, 3, 0:nq], 1.0 / 96, EPS,
                                                              ALU.mult, ALU.add), [KS], [KS])
                    op('pool', lambda: nc.gpsimd.tensor_tensor(s_[:, 6, 0:nq], s_[:, 4, 0:nq], nhalf[:, 0:nq], ALU.pow),
                       [KS, 'nhalf'], [KS])
                    op('dve', lambda: nc.vector.tensor_tensor(o2, o1, bc(s_[:, 6, 0:nq]), ALU.mult), [KO1, KS], [KO2])
                    op('dve', lambda: nc.vector.tensor_tensor(
                        yin[:, qt0:qt0 + nq, 384 + hd * 96:384 + (hd + 1) * 96], o2,
                        gsub[:].unsqueeze(1).broadcast_to([128, nq, 96]), ALU.mult),
                       [KO2, 'gsub'], [('yin', 1, G)])

                for s in range(len(units) + DLAG):
                    if s < len(units):
                        df_front(s)
                    if s - DLAG >= 0:
                        df_back(s - DLAG)
                    if s % 4 == 1:
                        pump_casts(l + 1, 1)
                pump_casts(l + 1, 100)
            S.barrier()

            if stop_after == 'S3':
                break
            with contextlib.ExitStack() as st:
                def sb2(name, shape, dt):
                    return st.enter_context(nc.sbuf_tensor("s_%s_%d" % (name, l), list(shape), dt))
                wab = W3[:, :, 0:512]
                uT = sb2("uT", [128, 2, U_LEN], BF16)
                cw = sb2("cw", [128, 2, 31], F32)
                cb = sb2("cb", [128, 2], F32)
                diag = sb2("diag", [128, 2, 31, 128], BF16)
                lng = sb2("lng", [128, 256], F32)
                lnb = sb2("lnb", [128, 256], F32)
                sig = [sb2("sig%d" % i, [128, 512], F32) for i in range(2)]
                ycs = [sb2("ycs%d" % i, [128, 512], F32) for i in range(4)]
                lst = [sb2("lst%d" % i, [128, 16], F32) for i in range(2)]
                yn = [sb2("yn%d" % i, [128, 256], F32) for i in range(2)]

                S.dma('sp', cw[:], conv_wT[l].rearrange("(cc p) k -> p cc k", p=128), writes=['cw'])
                for cc in range(2):
                    S.dma('sp', cb[:, cc:cc + 1], conv_b[l, cc * 128:(cc + 1) * 128].rearrange("(p o) -> p o", o=1), writes=['cb'])
                S.dma('sp', lng[:], ln_g[l, :].partition_broadcast(128), writes=['lng'])
                S.dma('sp', lnb[:], ln_b[l, :].partition_broadcast(128), writes=['lnb'])
                op('pool', lambda: nc.gpsimd.memset(uT[:], 0.0), [], ['uT'])
                cgroups = groups if upd else groups[:4]
                nxt = (l + 1 < n_layers)
                if nxt or l == 0:
                    bufs1 = s0a_bufs(st, "b")
                if l == 0:
                    s0a_load(0, 4, *bufs1, 0)
                    s0a_load(0, 5, *bufs1, 1)
                    s0a_compute(0, 4, *bufs1, 0)
                    s0a_compute(0, 5, *bufs1, 1)
                if nxt:
                    s0a_load(l + 1, 0, *bufs1, 0)
                    s0a_load(l + 1, 1, *bufs1, 1)
                s0n = [0]

                def s0a_step():
                    if nxt and s0n[0] < 6:
                        n_ = s0n[0]
                        s0a_compute(l + 1, n_, *bufs1, n_ % 2)
                        if n_ + 2 < 6:
                            s0a_load(l + 1, n_ + 2, *bufs1, n_ % 2)
                        s0n[0] += 1
                it = 0
                for cc in range(2):
                    for gi, (tok0, N) in enumerate(cgroups):
                        if it % 2 == 1:
                            s0a_step()
                        u0 = (U_LAT0 + tok0) if gi < 4 else U_CTX0
                        psa, psak = psf[0], PF[0]
                        psb, psbk = psf[1], PF[1]
                        mm_fm(psa, psak, 0, 128, wab, [W3K[0], W3K[1]], cc * 128, tok0, N)
                        mm_fm(psb, psbk, 0, 128, wab, [W3K[0], W3K[1]], 256 + cc * 128, tok0, N)
                        sg = sig[it % 2]
                        op('act', lambda: nc.scalar.activation(sg[:, 0:N], psb[:, 0:N], AF.Sigmoid), [psbk], [('sig', it % 2)])
                        op('dve', lambda: nc.vector.tensor_tensor(uT[:, cc, u0:u0 + N], psa[:, 0:N], sg[:, 0:N], ALU.mult),
                           [psak, ('sig', it % 2), 'uT'], [('uT', cc, gi)])
                        it += 1
                for cc in range(2):
                    for k in range(31):
                        op('dve', lambda: nc.vector.tensor_scalar(diag[:, cc, k, :], identb[:], cw[:, cc, k:k + 1], None, ALU.mult),
                           ['identb', 'cw'], ['diag'])
                S.dma('sp', W3[:, :, 0:384], wview(wb_in[l])[:, :, C_NAG:C_NAG + 384], reads=WBI(l, 5), writes=[W3K[0]])
                S.dma('sp', W3[:, :, 384:768], wview(wb_in[l])[:, :, C_DFG:C_DFG + 384], reads=WBI(l, 6), writes=[W3K[1]])
                S.dma('sp', W3[:, :, 768:1024], wview(wb_in[l])[:, :, C_CVG:C_CVG + 256], reads=WBI(l, 7), writes=[W3K[2]])
                s0b(l, [2], outer=st)
                cnt_it = [0]

                def cv_X(gi):
                    tok0, N = cgroups[gi]
                    u0 = (U_LAT0 + tok0) if gi < 4 else U_CTX0
                    for cc in range(2):
                        bi = 2 * (gi % 2) + cc
                        ps, psk = psf[bi], PF[bi]
                        ukeys = ['uT'] + [('uT', cc, g2) for g2 in range(len(cgroups)) if abs(g2 - gi) <= 1]
                        for k in range(31):
                            op('pe', lambda: nc.tensor.matmul(ps[:, 0:N], lhsT=diag[:, cc, k, :],
                                                              rhs=uT[:, cc, u0 - UPAD + k:u0 - UPAD + k + N],
                                                              start=(k == 0), stop=(k == 30)),
                               ['diag'] + ukeys, [psk], inc=(k == 30))
                        op('act', lambda: nc.scalar.activation(ycs[bi][:, 0:N], ps[:, 0:N], AF.Identity, bias=cb[:, cc:cc + 1]),
                           [psk, 'cb'], [('ycs', bi)])

                def cv_Y(gi):
                    tok0, N = cgroups[gi]
                    for tt in range(N // 128):
                        t = tok0 // 128 + tt
                        b = cnt_it[0] % 2
                        cnt_it[0] += 1
                        pst, pstk = psf[4 + b], PF[4 + b]
                        for cc in range(2):
                            bi = 2 * (gi % 2) + cc
                            op('pe', lambda: nc.tensor.transpose(pst[:, cc * 128:(cc + 1) * 128],
                                                                 ycs[bi][:, tt * 128:(tt + 1) * 128], identf[:]),
                               [('ycs', bi), 'identf'], [pstk], inc=(cc == 1))
                        ls = lst[b]
                        LK = ('lst', b)
                        op('dve', lambda: nc.vector.bn_stats(ls[:, 0:6], pst[:, 0:256]), [pstk], [LK])
                        op('dve', lambda: nc.vector.bn_aggr(ls[:, 6:8], ls[:, 0:6]), [LK], [LK])
                        op('dve', lambda: nc.vector.tensor_scalar(ls[:, 8:9], ls[:, 7:8], EPS, None, ALU.add), [LK], [LK])
                        op('pool', lambda: nc.gpsimd.tensor_tensor(ls[:, 10:11], ls[:, 8:9], nhalf[:, 0:1], ALU.pow),
                           [LK, 'nhalf'], [LK])
                        y_ = yn[b]
                        YK = ('yn', b)
                        op('dve', lambda: nc.vector.tensor_scalar(y_[:], pst[:, 0:256], ls[:, 6:7], ls[:, 10:11],
                                                                  ALU.subtract, ALU.mult), [pstk, LK], [YK])
                        op('dve', lambda: nc.vector.tensor_tensor(y_[:], y_[:], lng[:], ALU.mult), [YK, 'lng'], [YK])
                        op('dve', lambda: nc.vector.tensor_tensor(y_[:], y_[:], lnb[:], ALU.add), [YK, 'lnb'], [YK])
                        op('act', lambda: nc.scalar.activation(yin[:, t, 768:1024], y_[:], AF.Silu), [YK],
                           [('yin', 2, t)])

                for s in range(len(cgroups) + 1):
                    if s < len(cgroups):
                        cv_X(s)
                    if s - 1 >= 0:
                        cv_Y(s - 1)
                    s0a_step()
                while nxt and s0n[0] < 6:
                    s0a_step()
            S.barrier()

            if stop_after == 'S4':
                break
            if debug:
                for t in range(NT):
                    S.dma('sp', dbg_yin[l, t * 128:(t + 1) * 128, :], yin[:, t, :], reads=[], writes=[('dbgy', t)])

            with contextlib.ExitStack() as st:
                def sb2(name, shape, dt):
                    return st.enter_context(nc.sbuf_tensor("s_%s_%d" % (name, l), list(shape), dt))
                wg = W3[:, :, 0:D]
                wo = sb2("wo", [128, 8, D], BF16)
                xt = [sb2("fxt%d" % i, [128, D], F32) for i in range(2)]
                gs = [sb2("gs%d" % i, [128, D], BF16) for i in range(2)]
                yg = [sb2("yg%d" % i, [128, D], BF16) for i in range(2)]
                yT = [sb2("yT%d" % i, [128, 8, 128], BF16) for i in range(2)]
                tmp = [sb2("ftmp%d" % i, [128, D], F32) for i in range(2)]
                xo = [sb2("xo%d" % i, [128, D], F32) for i in range(2)]
                fst = sb2("fst", [128, NT, 8], F32)
                S.dma('sp', wo[:], wview(wb_out[l])[:, :, :], reads=WBO(l), writes=['wo'])
                nxt = (l + 1 < n_layers)
                if nxt:
                    s0b(l + 1, [0, 1], outer=st)
                    ntmpf = [sb2("ntmpf%d" % i, [128, D], F32) for i in range(2)]
                    nhb = [sb2("nhb%d" % i, [128, D], BF16) for i in range(4)]
                    nstat = sb2("nstat", [128, 4, NT], F32)
                def f_A(t):
                    b = t % 2
                    for half in range(2):
                        ps, psk = psf[half], PF[half]
                        mm_tm(ps, psk, t, wg, W3K, half * 512, 512)
                        op('act', lambda: nc.scalar.activation(gs[b][:, half * 512:(half + 1) * 512], ps[:, :], AF.Silu),
                           [psk], [('gs', b, half)])
                    op('dve', lambda: nc.vector.tensor_tensor(yg[b][:], yin[:, t, :], gs[b][:], ALU.mult),
                       [('gs', b, 0), ('gs', b, 1)], [('yg', b)])

                def f_B(t):
                    b = t % 2
                    S.dma('sp', xt[b][:], Xsrc(t), reads=xreads(t), writes=[('fxt', b)])
                    for kc in range(8):
                        op('pe', lambda: nc.tensor.transpose(psT[b][:, kc * 128:(kc + 1) * 128],
                                                             yg[b][:, kc * 128:(kc + 1) * 128], identb[:]),
                           [('yg', b), 'identb'], [PT[b]], inc=(kc == 7))
                    evac_copy(yT[b][:], psT[b][:].rearrange("p (a b) -> p a b", b=128), [PT[b]], [('yT', b)])

                def f_C(t):
                    b = t % 2
                    which = 0 if t < 16 else 1
                    f = fst[:, t, :]
                    FK = ('fst', t)
                    for half in range(2):
                        ps, psk = psf[2 + 2 * b + half], PF[2 + 2 * b + half]
                        for kc in range(8):
                            op('pe', lambda: nc.tensor.matmul(ps[:, :], lhsT=yT[b][:, kc, :],
                                                              rhs=wo[:, kc, half * 512:(half + 1) * 512],
                                                              start=(kc == 0), stop=(kc == 7)),
                               [('yT', b), 'wo'], [psk], inc=(kc == 7))
                        op('act', lambda: nc.scalar.activation(junk[:, 0:512], ps[:, :], AF.Square, accum_out=f[:, half:half + 1]),
                           [psk], ['junk', (FK, half)])
                        op('dve', lambda: nc.vector.tensor_tensor(tmp[b][:, half * 512:(half + 1) * 512], ps[:, :],
                                                                  tabG[:, which, half * 512:(half + 1) * 512], ALU.mult),
                           [psk], [('ftmp', b, half)])
                    op('dve', lambda: nc.vector.tensor_tensor(f[:, 2:3], f[:, 0:1], f[:, 1:2], ALU.add),
                       [(FK, 0), (FK, 1)], [FK])
                    op('dve', lambda: nc.vector.tensor_scalar(f[:, 3:4], f[:, 2:3], 1.0 / D, EPS, ALU.mult, ALU.add), [FK], [FK])
                    op('pool', lambda: nc.gpsimd.tensor_tensor(f[:, 5:6], f[:, 3:4], nhalf[:, 0:1], ALU.pow),
                       [FK, 'nhalf'], [FK])
                    op('dve', lambda: nc.vector.scalar_tensor_tensor(xo[b][:], tmp[b][:], f[:, 5:6], xt[b][:], ALU.mult, ALU.add),
                       [('ftmp', b, 0), ('ftmp', b, 1), FK, ('fxt', b)], [('xo', b)])
                    if last:
                        S.dma('sp', out[t * 128:(t + 1) * 128, :], xo[b][:], reads=[('xo', b)], writes=[('out', t)])
                    else:
                        S.dma('sp', Xs[t * 128:(t + 1) * 128, :], xo[b][:], reads=[('xo', b)], writes=[('Xs', t)])
                    if debug:
                        S.dma('sp', dbg_x[l, t * 128:(t + 1) * 128, :], xo[b][:], reads=[('xo', b)], writes=[('dbgx', t)])
                    if nxt:
                        prenorm_front(t, xo[b][:], ('xo', b), (ntmpf[b], ('ntmpf', b)), (nhb[t % 4], ('nhb', t % 4)), nstat)

                def f_D(t):
                    prenorm_back(t, (nhb[t % 4], ('nhb', t % 4)), t % 2)

                for s in range(ntile_q + 5):
                    if s < ntile_q:
                        f_A(s)
                    if 0 <= s - 1 < ntile_q:
                        f_B(s - 1)
                    if nxt and 0 <= s - LAGD < ntile_q:
                        f_D(s - LAGD)
                    if 0 <= s - 2 < ntile_q:
                        f_C(s - 2)
                    if s == ntile_q - 1 and nxt:
                        load_kqv(l + 1, C_NAK, C_NAQ, C_NAV, 0, 1, 0)
            S.barrier()

        if debug and stop_after is not None:
            for t in range(NT):
                S.dma('sp', dbg_yin[0, t * 128:(t + 1) * 128, :], yin[:, t, :], reads=[], writes=[('dbgy', t)])
        S.finish()
    return nc


def _host_constants():
    ident = np.eye(128, dtype=np.float32)
    selc = np.zeros((2, 2, 128), np.float32)
    selc[0, 0, :] = 1.0
    selc[1, 1, :] = 1.0
    t = np.arange(T)
    rows = (t // 64).astype(np.float32)
    cols = (t % 64).astype(np.float32)
    inv = (10000.0 ** (-np.arange(0, 24, 2, dtype=np.float32) / 24.0)).astype(np.float32)
    ropec = np.zeros((128, T), np.float32)
    ropes = np.zeros((128, T), np.float32)
    perm = np.zeros((128, 128), np.float32)
    for mm in range(2):
        for d in range(48):
            p = mm * 64 + d
            pos = rows if d < 24 else cols
            dd = d % 24
            i = dd % 12
            ang = (pos * inv[i]).astype(np.float32)
            ropec[p] = np.cos(ang)
            s = np.sin(ang)
            ropes[p] = -s if dd < 12 else s
            partner = d + 12 if dd < 12 else d - 12
            perm[mm * 64 + partner, p] = 1.0
    kr_l = np.arange(128) // 64
    kc = np.arange(128) % 64
    qc = np.arange(64)
    cstart = np.clip(qc - 8, 0, 48)
    colvalid = (kc[:, None] >= cstart[None, :]) & (kc[:, None] < cstart[None, :] + 16)
    dc_idx = np.clip(kc[:, None] - qc[None, :] + 15, 0, 30)
    dr_idx = np.zeros((128, NBLK * 64), np.int64)
    dc_all = np.zeros((128, NBLK * 64), np.int64)
    mask = np.zeros((128, NBLK * 64), np.float32)
    for b in range(NBLK):
        e = 6 - b if b < 14 else 4 - (b - 14)
        dr = e + kr_l
        rowvalid = np.ones(128, bool) if b < 14 else ((dr >= -4) & (dr <= 3))
        valid = colvalid & rowvalid[:, None]
        sl = slice(b * 64, (b + 1) * 64)
        dr_idx[:, sl] = np.clip(dr + 7, 0, 14)[:, None]
        dc_all[:, sl] = dc_idx
        mask[:, sl] = np.where(valid, 0.0, NEG)
    return dict(ident=ident, selc=selc, ropec=ropec, ropes=ropes, perm=perm,
                nab_mask=mask, zpad=np.zeros((1024, 64), np.float32)), dr_idx, dc_all


_NC_CACHE = {}


def _prep_inputs(inputs, batch_ids):
    consts, dr_idx, dc_all = _host_constants()
    f = lambda a: np.ascontiguousarray(np.asarray(a, dtype=np.float32))
    rpb = f(inputs['na_rpb'])
    nab_val = np.ascontiguousarray(rpb[:, :, dr_idx, dc_all])
    lqk = np.ascontiguousarray(np.stack([f(inputs['diff_lq1']), f(inputs['diff_lk1']),
                                         f(inputs['diff_lq2']), f(inputs['diff_lk2'])], axis=1))
    shared = dict(
        w_mod=f(inputs['w_mod']), b_mod=f(inputs['b_mod']), g_pre=f(inputs['g_pre']), g_post=f(inputs['g_post']),
        w_in=f(inputs['w_in']), w_out=f(inputs['w_out']), nab_val=nab_val, lqk=lqk,
        subln_g=f(inputs['diff_subln_g']), conv_wT=np.ascontiguousarray(f(inputs['conv_w']).transpose(0, 2, 1)),
        conv_b=f(inputs['conv_b']), ln_g=f(inputs['conv_ln_g']), ln_b=f(inputs['conv_ln_b']),
        cctxT=np.ascontiguousarray(f(inputs['c_ctx']).reshape(8, 128).T),
    )
    shared.update(consts)
    x = f(inputs['x'])
    ctx = f(inputs['ctx'])
    c = f(inputs['c'])
    maps = []
    for b in batch_ids:
        m = dict(shared)
        m['x'] = x[b]
        m['ctx'] = ctx[b]
        m['cT'] = np.ascontiguousarray(c[b].reshape(8, 128).T)
        maps.append(m)
    return maps


def kernel(**inputs):
    if 'full' not in _NC_CACHE:
        _NC_CACHE['full'] = build_nc(DEPTH, debug=False)
    nc = _NC_CACHE['full']
    maps = _prep_inputs(inputs, list(range(8)))
    res = run_bass_kernel_spmd(nc, maps, core_ids=list(range(8)))
    return np.stack([np.asarray(r['out'], dtype=np.float32) for r in res.results], axis=0)
```

```python
import bisect
import numpy as np
import concourse.bass as bass
import concourse.mybir as mybir
from concourse.bass_utils import run_bass_kernel_spmd

F32 = mybir.dt.float32
BF16 = mybir.dt.bfloat16
AF = mybir.ActivationFunctionType
ALU = mybir.AluOpType
AX = mybir.AxisListType


class Sched:
    NDMA_SEMS = 8

    def __init__(self, nc, self_sync=True):
        self.nc = nc
        self.self_sync = self_sync
        self.handles = {'pe': nc.tensor, 'act': nc.scalar, 'dve': nc.vector,
                        'pool': nc.gpsimd, 'sp': nc.sync}
        self._ctx = []

    def begin(self):
        nc = self.nc
        self.sem = {}
        self.cnt = {}
        self.seen = {}
        self.hist = {}
        self.semobj = {}
        for e in self.handles:
            cm = nc.semaphore("sem_" + e)
            s = cm.__enter__()
            self._ctx.append(cm)
            self.sem[e] = "E_" + e
            self.semobj["E_" + e] = s
            self.cnt[e] = 0
            self.seen[e] = {}
            self.hist["E_" + e] = ([], [])
        self.dsem = {}
        self.dcnt = {}
        self.dnext = {}
        for q in ('sp', 'pool', 'act'):
            names = []
            for i in range(self.NDMA_SEMS):
                cm = nc.semaphore("dsem_%s_%d" % (q, i))
                s = cm.__enter__()
                self._ctx.append(cm)
                nm = "D_%s_%d" % (q, i)
                self.semobj[nm] = s
                self.hist[nm] = ([], [])
                self.dcnt[nm] = 0
                names.append(nm)
            self.dsem[q] = names
            self.dnext[q] = 0
        self.lastw = {}
        self.readers = {}

    def _closure(self, tok):
        nm, v = tok
        vals, snaps = self.hist[nm]
        i = bisect.bisect_left(vals, v)
        if i < len(vals):
            return snaps[i]
        return {}

    def _wait(self, e, toks):
        need = {}
        for tok in toks:
            if tok is None:
                continue
            nm, v = tok
            if need.get(nm, 0) < v:
                need[nm] = v
        seen = self.seen[e]
        own = self.sem[e]
        items = [(nm, v) for nm, v in need.items() if seen.get(nm, 0) < v]
        items.sort(key=lambda t: -len(self._closure(t)))
        h = self.handles[e]
        for nm, v in items:
            if seen.get(nm, 0) >= v:
                continue
            h.wait_ge(self.semobj[nm], v)
            seen[nm] = v
            for k2, v2 in self._closure((nm, v)).items():
                if seen.get(k2, 0) < v2:
                    seen[k2] = v2

    def _deps(self, e, reads, writes):
        own = self.sem[e]
        toks = []
        for k in reads:
            t = self.lastw.get(k)
            if t is not None:
                if not (t[0] == own and not self.self_sync):
                    toks.append(t)
            if isinstance(k, tuple) and isinstance(k[0], str) and k[0].startswith('ps'):
                r = self.readers.get(k)
                if r:
                    for t2 in r.values():
                        if t2[0] != own:
                            toks.append(t2)
        for k in writes:
            t = self.lastw.get(k)
            if t is not None and t[0] != own:
                toks.append(t)
            r = self.readers.get(k)
            if r:
                for t2 in r.values():
                    if t2[0] != own:
                        toks.append(t2)
        return toks

    def _record(self, e, tok, reads, writes):
        for k in writes:
            self.lastw[k] = tok
            self.readers[k] = {}
        for k in reads:
            self.readers.setdefault(k, {})[e] = tok

    def op(self, e, fn, reads=(), writes=(), inc=True, extra=()):
        toks = self._deps(e, reads, writes)
        toks.extend(extra)
        self._wait(e, toks)
        ins = fn()
        nm = self.sem[e]
        tok = (nm, self.cnt[e] + 1)
        if inc:
            ins.then_inc(self.semobj[nm], 1)
            self.cnt[e] += 1
            vals, snaps = self.hist[nm]
            vals.append(self.cnt[e])
            snaps.append(dict(self.seen[e]))
        self._record(e, tok, reads, writes)
        return tok

    def dma(self, q, out, in_, reads=(), writes=(), extra=()):
        toks = self._deps(q, reads, writes)
        toks.extend(extra)
        i = self.dnext[q]
        self.dnext[q] = (i + 1) % self.NDMA_SEMS
        nm = self.dsem[q][i]
        if self.dcnt[nm] > 0:
            toks.append((nm, 16 * self.dcnt[nm]))
        self._wait(q, toks)
        h = self.handles[q]
        ins = h.dma_start(out=out, in_=in_)
        ins.then_inc(self.semobj[nm], 16)
        self.dcnt[nm] += 1
        tok = (nm, 16 * self.dcnt[nm])
        vals, snaps = self.hist[nm]
        vals.append(tok[1])
        snaps.append(dict(self.seen[q]))
        self._record(q, tok, reads, writes)
        return tok

    def barrier(self):
        toks = []
        for e in self.handles:
            if self.cnt[e] > 0:
                toks.append((self.sem[e], self.cnt[e]))
        for nm, c in self.dcnt.items():
            if c > 0:
                toks.append((nm, 16 * c))
        for e in self.handles:
            self._wait(e, list(toks))
        self.lastw = {}
        self.readers = {}

    def finish(self):
        self.barrier()
        for cm in reversed(self._ctx):
            cm.__exit__(None, None, None)
        self._ctx = []


D = 1024
T = 2048
CTX = 256
NTOK = T + CTX
NT = NTOK // 128
DEPTH = 4
EPS = 1e-6
NEG = -30000.0
import os
S5CUT = int(os.environ.get('S5CUT', '9'))
LAGR = 1
LAGD = 4
C_NAK, C_NAV, C_DFK, C_DFV = 0, 384, 768, 1152
C_NAQ, C_NAG, C_DFQ, C_DFG = 1536, 1920, 2304, 2688
C_CVA, C_CVB, C_CVG = 3072, 3328, 3584
NBLK = 24
UPAD = 15
U_LAT0 = UPAD
U_CTX0 = UPAD + T + 2 * UPAD
U_LEN = U_CTX0 + CTX + UPAD


def _na_rows(j):
    lo = max(4, 2 * j - 4)
    hi = min(27, 2 * j + 5)
    rows = list(range(lo, hi + 1))
    if j <= 3:
        rows = [0, 1, 2, 3] + rows
    if j >= 12:
        rows = rows + [28, 29, 30, 31]
    return rows


def _na_block(j, r):
    e = 2 * j - r
    if r <= 3 or r >= 28:
        b = 6 - e
        assert 0 <= b < 14
        return b
    b = 14 + (4 - e)
    assert 14 <= b < 24
    return b


def build_nc(n_layers=DEPTH, debug=False, stop_after=None):
    nc = bass.Bass("TRN2", target_bir_lowering=False)

    def din(name, shape, dt=F32):
        return nc.dram_tensor(name, list(shape), dt, kind="ExternalInput").ap()

    x_in = din("x", [T, D])
    ctx_in = din("ctx", [CTX, D])
    cT_in = din("cT", [128, 8])
    cctxT_in = din("cctxT", [128, 8])
    w_mod = din("w_mod", [DEPTH, D, 3 * D])
    b_mod = din("b_mod", [DEPTH, 3 * D])
    g_pre = din("g_pre", [DEPTH, D])
    g_post = din("g_post", [DEPTH, D])
    w_in = din("w_in", [DEPTH, D, 3840])
    w_out = din("w_out", [DEPTH, D, D])
    nab_val = din("nab_val", [DEPTH, 6, 128, NBLK * 64])
    nab_mask = din("nab_mask", [128, NBLK * 64])
    lqk = din("lqk", [DEPTH, 4, 48])
    subln_g = din("subln_g", [DEPTH, 96])
    conv_wT = din("conv_wT", [DEPTH, 256, 31])
    conv_b = din("conv_b", [DEPTH, 256])
    ln_g = din("ln_g", [DEPTH, 256])
    ln_b = din("ln_b", [DEPTH, 256])
    ident_in = din("ident", [128, 128])
    selc_in = din("selc", [2, 2, 128])
    ropec_in = din("ropec", [128, T])
    ropes_in = din("ropes", [128, T])
    perm_in = din("perm", [128, 128])
    out = nc.dram_tensor("out", [T, D], F32, kind="ExternalOutput").ap()
    Xs = nc.dram_tensor("Xs", [NTOK, D], F32, kind="Internal").ap()
    rows_d = nc.dram_tensor("rows_d", [DEPTH, 2, 3 * D], F32, kind="Internal").ap()
    wbp = [nc.dram_tensor("wbp%d" % l, [D, 1024], BF16, kind="Internal").ap() for l in range(DEPTH)]
    wb_mod = [nc.dram_tensor("wb_mod%d" % l, [D, 3 * D], BF16, kind="Internal").ap() for l in range(DEPTH)]
    zpad_in = din("zpad", [1024, 64])
    wb_in = [nc.dram_tensor("wb_in%d" % l, [D, 3840], BF16, kind="Internal").ap() for l in range(DEPTH)]
    wb_out = [nc.dram_tensor("wb_out%d" % l, [D, D], BF16, kind="Internal").ap() for l in range(DEPTH)]
    if debug:
        dbg_yin = nc.dram_tensor("dbg_yin", [n_layers, NTOK, D], BF16, kind="ExternalOutput").ap()
        dbg_x = nc.dram_tensor("dbg_x", [n_layers, NTOK, D], F32, kind="ExternalOutput").ap()

    S = Sched(nc)
    op = S.op

    def wview(ap2d):
        return ap2d.rearrange("(kc p) n -> p kc n", p=128)

    import contextlib
    es = contextlib.ExitStack()

    def sb(name, shape, dt):
        return es.enter_context(nc.sbuf_tensor("s_" + name, list(shape), dt))

    with es:
        PP = [es.enter_context(nc.psum_tensor("pp%d" % i, [128, 1024], F32)) for i in range(4)]
        psT = [PP[0][:, i * 512:(i + 1) * 512].bitcast(BF16) for i in range(2)]
        psf = [PP[1 + i // 2][:, (i % 2) * 512:(i % 2 + 1) * 512] for i in range(6)]
        PT = [('psT', i) for i in range(2)]
        PF = [('psf', i) for i in range(6)]

        identf = sb("identf", [128, 128], F32)
        identb = sb("identb", [128, 128], BF16)
        permf = sb("permf", [128, 128], F32)
        permb = sb("permb", [128, 128], BF16)
        selc = sb("selc", [2, 2, 128], F32)
        cT = sb("cTs", [128, 2, 8], F32)
        sc2 = sb("sc2", [128, 8, 2], F32)
        sc2b = sb("sc2b", [128, 8, 2], BF16)
        hT = sb("hT", [128, 8, NTOK], BF16)
        yin = sb("yin", [128, NT, D], BF16)
        tabG = sb("tabG", [128, 2, D], F32)
        tabAB = sb("tabAB", [128, 2, 2, D], F32)
        junk = sb("junk", [128, D], BF16)
        W3 = sb("W3", [128, 8, 1408], BF16)
        nhalf = sb("nhalf", [128, 32], F32)

        S.begin()
        op_ = S.op
        op_('pool', lambda: nc.gpsimd.memset(nhalf[:], -0.5), [], ['nhalf'])

        cast_q = {l_: [] for l_ in range(n_layers)}
        CGROUPS = [(0, 768), (1536, 384), (1152, 384), None, (3072, 512),
                   (1920, 384), (2688, 384), (3584, 256)]
        for l in range(n_layers):
            for gi_, cg_ in enumerate(CGROUPS):
                for rh in range(2):
                    rs = slice(rh * 512, (rh + 1) * 512)
                    if cg_ is not None:
                        c0_, n_ = cg_
                        cast_q[l].append((wb_in[l][rs, c0_:c0_ + n_], w_in[l, rs, c0_:c0_ + n_], ('wbi', l, gi_, rh)))
                    elif rh == 0:
                        for off_, c0_ in ((0, C_DFK), (512, C_DFQ)):
                            for m_ in range(8):
                                cast_q[l].append((wbp[l][:, off_ + m_ * 64:off_ + m_ * 64 + 48],
                                                  w_in[l, :, c0_ + m_ * 48:c0_ + (m_ + 1) * 48],
                                                  ('wbp', l, off_, m_)))
                                cast_q[l].append((wbp[l][:, off_ + m_ * 64 + 48:off_ + (m_ + 1) * 64],
                                                  zpad_in[:, 0:16], ('wbpz', l, off_, m_)))
            for rh in range(2):
                rs = slice(rh * 512, (rh + 1) * 512)
                cast_q[l].append((wb_out[l][rs, :], w_out[l, rs, :], ('wbo', l, rh)))
            if l >= 1:
                for rh in range(2):
                    rs = slice(rh * 512, (rh + 1) * 512)
                    for cp_ in range(3):
                        cast_q[l].append((wb_mod[l][rs, cp_ * 1024:(cp_ + 1) * 1024],
                                          w_mod[l, rs, cp_ * 1024:(cp_ + 1) * 1024], ('wbm', l, rh, cp_)))

        def pump_casts(l_, k=1):
            if l_ >= n_layers:
                return
            for _ in range(k):
                if cast_q[l_]:
                    o_, i_, key_ = cast_q[l_].pop(0)
                    S.dma('pool', o_, i_, writes=[key_])
        pump_casts(0, 4)

        def WBI(l, *gis):
            ks = []
            for g_ in gis:
                if g_ == 3:
                    for off_ in (0, 512):
                        for m_ in range(8):
                            ks += [('wbp', l, off_, m_), ('wbpz', l, off_, m_)]
                else:
                    ks += [('wbi', l, g_, rh) for rh in range(2)]
            return ks
        def WBO(l): return [('wbo', l, rh) for rh in range(2)]

        S.dma('sp', identf[:], ident_in[:, :], writes=['identf'])
        S.dma('sp', permf[:], perm_in[:, :], writes=['permf'])
        S.dma('sp', selc[:], selc_in[:, :, :], writes=['selc'])
        S.dma('sp', cT[:, 0, :], cT_in[:, :], writes=['cT'])
        S.dma('sp', cT[:, 1, :], cctxT_in[:, :], writes=['cT'])
        op('dve', lambda: nc.vector.tensor_copy(identb[:], identf[:]), ['identf'], ['identb'])
        op('dve', lambda: nc.vector.tensor_copy(permb[:], permf[:]), ['permf'], ['permb'])
        op('act', lambda: nc.scalar.activation(sc2[:, :, 0], cT[:, 0, :], AF.Silu), ['cT'], ['sc2'])
        op('act', lambda: nc.scalar.activation(sc2[:, :, 1], cT[:, 1, :], AF.Silu), ['cT'], ['sc2'])
        op('dve', lambda: nc.vector.tensor_copy(sc2b[:], sc2[:]), ['sc2'], ['sc2b'])

        def hkeys(tok0, n):
            return [('hT', t) for t in range(tok0 // 128, (tok0 + n + 127) // 128)]

        cpy_rr = [0]

        def evac_copy(dst, src, reads, writes, scale=None):
            cpy_rr[0] ^= 1
            if cpy_rr[0]:
                if scale is None:
                    op('act', lambda: nc.scalar.copy(dst, src), reads, writes)
                else:
                    op('act', lambda: nc.scalar.activation(dst, src, AF.Copy, scale=scale), reads, writes)
            else:
                if scale is None:
                    op('dve', lambda: nc.vector.tensor_copy(dst, src), reads, writes)
                else:
                    op('dve', lambda: nc.vector.tensor_scalar(dst, src, scale, None, ALU.mult), reads, writes)

        for l in range(n_layers):
            last = (l == DEPTH - 1)
            upd = not last
            lam_init = 0.8 - 0.6 * float(np.exp(-0.3 * l))
            ntile_q = NT if upd else 16
            Xsrc = (lambda t: (x_in[t * 128:(t + 1) * 128, :] if t < 16 else ctx_in[(t - 16) * 128:(t - 15) * 128, :])) \
                if l == 0 else (lambda t: Xs[t * 128:(t + 1) * 128, :])
            xreads = (lambda t: []) if l == 0 else (lambda t: [('Xs', t)])

            def s0a_load(l2, n, wm, bmp, gpp, rowp, bi):
                if l2 >= 1:
                    S.dma('sp', wm[bi][:], wview(wb_mod[l2])[:, :, n * 512:(n + 1) * 512],
                          reads=[('wbm', l2, rh_, n // 2) for rh_ in range(2)], writes=[('wm', bi)])
                else:
                    S.dma('sp', wm[bi][:], wview(w_mod[l2])[:, :, n * 512:(n + 1) * 512], writes=[('wm', bi)])

            def s0a_compute(l2, n, wm, bmp, gpp, rowp, bi):
                ti_src, half = n // 2, n % 2
                for p in range(2):
                    S.dma('sp', bmp[0][p:p + 1, :], b_mod[l2:l2 + 1, n * 512:(n + 1) * 512], writes=[('bmp', 0)])
                    if ti_src == 1:
                        S.dma('sp', gpp[0][p:p + 1, :], g_pre[l2:l2 + 1, half * 512:(half + 1) * 512], writes=[('gpp', 0)])
                    elif ti_src == 2:
                        S.dma('sp', gpp[0][p:p + 1, :], g_post[l2:l2 + 1, half * 512:(half + 1) * 512], writes=[('gpp', 0)])
                w = wm[bi]
                ps, psk = psf[4 + bi], PF[4 + bi]
                for kc in range(8):
                    scx = sc2b if l2 >= 1 else sc2
                    op('pe', lambda: nc.tensor.matmul(ps[0:2, :], lhsT=scx[:, kc, :], rhs=w[:, kc, :],
                                                      start=(kc == 0), stop=(kc == 7)),
                       ['sc2', 'sc2b', ('wm', bi)], [psk], inc=(kc == 7))
                rp = rowp[bi]
                op('dve', lambda: nc.vector.tensor_tensor(rp[0:2, :], ps[0:2, :], bmp[bi][0:2, :], ALU.add),
                   [psk, ('bmp', 0)], [('rowp', 0)])
                if ti_src == 1:
                    op('dve', lambda: nc.vector.scalar_tensor_tensor(rp[0:2, :], rp[0:2, :], 1.0, gpp[bi][0:2, :],
                                                                     ALU.add, ALU.mult), [('rowp', 0), ('gpp', 0)], [('rowp', 0)])
                elif ti_src == 2:
                    op('dve', lambda: nc.vector.tensor_tensor(rp[0:2, :], rp[0:2, :], gpp[bi][0:2, :], ALU.mult),
                       [('rowp', 0), ('gpp', 0)], [('rowp', 0)])
                dst = {0: 1, 1: 0, 2: 2}[ti_src]
                S.dma('sp', rows_d[l2, :, dst * D + half * 512:dst * D + (half + 1) * 512], rp[0:2, :],
                      reads=[('rowp', 0)], writes=[('rows_d', l2, n)])

            def s0a_piece(l2, n, wm, bmp, gpp, rowp, bi):
                s0a_load(l2, n, wm, bmp, gpp, rowp, bi)
                s0a_compute(l2, n, wm, bmp, gpp, rowp, bi)

            def s0a_bufs(st_, tag):
                def sbx(name, shape, dt):
                    return st_.enter_context(nc.sbuf_tensor("s_%s_%s_%d" % (name, tag, l), list(shape), dt))
                wm = [sbx("wm%d" % i, [128, 8, 512], BF16 if tag == "b" else F32) for i in range(2)]
                bmp = [sbx("bmp0", [2, 512], F32)] * 2
                gpp = [sbx("gpp0", [2, 512], F32)] * 2
                rowp = [sbx("rowp0", [2, 512], F32)] * 2
                return wm, bmp, gpp, rowp

            def s0b(l2, tis, outer=None):
                with contextlib.ExitStack() as st_:
                    t0_ = tis[0]
                    nt_ = len(tis)
                    rows2 = (outer or st_).enter_context(
                        nc.sbuf_tensor("s_rows2_%d_%d_%d" % (l, l2, t0_), [2, nt_ * D], F32))
                    S.dma('sp', rows2[:], rows_d[l2, :, t0_ * D:(t0_ + nt_) * D],
                          reads=[('rows_d', l2, n) for n in range(6)], writes=['rows2'])
                    k = 0
                    for ti in tis:
                        for which in range(2):
                            for half in range(2):
                                ps = psf[2 + (k % 2)]
                                c0_ = (ti - t0_) * D + half * 512
                                op('pe', lambda: nc.tensor.matmul(ps[:, :], lhsT=selc[0:2, which, :],
                                                                  rhs=rows2[0:2, c0_:c0_ + 512],
                                                                  start=True, stop=True),
                                   ['selc', 'rows2'], [PF[2 + (k % 2)]])
                                dstt = tabG[:, which, :] if ti == 2 else tabAB[:, ti, which, :]
                                evac_copy(dstt[:, half * 512:(half + 1) * 512], ps[:, :],
                                          [PF[2 + (k % 2)]], [('tabs', ti, which)])
                                k += 1
                    if outer is None:
                        S.barrier()

            def prenorm_front(t, xsrc_ap, xkey, tmpf_, hb_, stat_, use_pool=True):
                which = 0 if t < 16 else 1
                op('act', lambda: nc.scalar.activation(junk[:], xsrc_ap, AF.Square, accum_out=stat_[:, 0, t:t + 1]),
                   [xkey], ['junk', ('stat', t)])
                op('dve', lambda: nc.vector.tensor_scalar(stat_[:, 1, t:t + 1], stat_[:, 0, t:t + 1], 1.0 / D, EPS,
                                                          ALU.mult, ALU.add), [('stat', t)], [('stat1', t)])
                if use_pool:
                    op('pool', lambda: nc.gpsimd.tensor_tensor(stat_[:, 3, t:t + 1], stat_[:, 1, t:t + 1], nhalf[:, 0:1], ALU.pow),
                       [('stat1', t), 'nhalf'], [('stat3', t)])
                else:
                    op('act', lambda: nc.scalar.activation(stat_[:, 2, t:t + 1], stat_[:, 1, t:t + 1], AF.Sqrt),
                       [('stat1', t)], [('stat2', t)])
                    op('dve', lambda: nc.vector.reciprocal(stat_[:, 3, t:t + 1], stat_[:, 2, t:t + 1]),
                       [('stat2', t)], [('stat3', t)])
                op('dve', lambda: nc.vector.scalar_tensor_tensor(tmpf_[0][:], xsrc_ap, stat_[:, 3, t:t + 1],
                                                                 tabAB[:, 0, which, :], ALU.mult, ALU.mult),
                   [xkey, ('stat3', t), ('tabs', 0, which)], [tmpf_[1]])
                if use_pool:
                    op('pool', lambda: nc.gpsimd.tensor_tensor(hb_[0][:], tmpf_[0][:], tabAB[:, 1, which, :], ALU.add),
                       [tmpf_[1], ('tabs', 1, which)], [hb_[1]])
                else:
                    op('dve', lambda: nc.vector.tensor_tensor(hb_[0][:], tmpf_[0][:], tabAB[:, 1, which, :], ALU.add),
                       [tmpf_[1], ('tabs', 1, which)], [hb_[1]])

            def prenorm_back(t, hb_, pb_):
                for kc in range(8):
                    op('pe', lambda: nc.tensor.transpose(psT[pb_][:, kc * 128:(kc + 1) * 128],
                                                         hb_[0][:, kc * 128:(kc + 1) * 128], identb[:]),
                       [hb_[1], 'identb'], [PT[pb_]], inc=(kc == 7))
                evac_copy(hT[:, :, t * 128:(t + 1) * 128], psT[pb_][:].rearrange("p (a b) -> p a b", b=128),
                          [PT[pb_]], [('hT', t)])

            def prenorm_tile(t, xsrc_ap, xkey, tmpf_, hb_, stat_, pb_):
                prenorm_front(t, xsrc_ap, xkey, tmpf_, hb_, stat_, use_pool=False)
                prenorm_back(t, hb_, pb_)

            if l == 0:
                with contextlib.ExitStack() as st:
                    bufs0 = s0a_bufs(st, "a")
                    s0a_load(0, 0, *bufs0, 0)
                    for n in range(6):
                        if n + 1 < 6:
                            s0a_load(0, n + 1, *bufs0, (n + 1) % 2)
                        s0a_compute(0, n, *bufs0, n % 2)
                S.barrier()
                s0b(0, [0, 1, 2])
                with contextlib.ExitStack() as st:
                    def sb2(name, shape, dt):
                        return st.enter_context(nc.sbuf_tensor("s_%s_%d" % (name, l), list(shape), dt))
                    NB1 = 4
                    xt = [sb2("xt%d" % i, [128, D], F32) for i in range(NB1)]
                    tmpf = [sb2("tmpf%d" % i, [128, D], F32) for i in range(NB1)]
                    hb = [sb2("hb%d" % i, [128, D], BF16) for i in range(NB1)]
                    stat = sb2("stat", [128, 4, NT], F32)
                    for t in range(NT):
                        b = t % NB1
                        S.dma('sp', xt[b][:], Xsrc(t), reads=xreads(t), writes=[('xt', b)])
                        prenorm_tile(t, xt[b][:], ('xt', b), (tmpf[b], ('tmpf', b)), (hb[b], ('hb', b)), stat, t % 2)
                S.barrier()
            if stop_after in ('S0', 'S1'):
                break

            def mm_fm(ps, pskey, pbase, M, w, wkey, c0, tok0, N, inc_last=True):
                for kc in range(8):
                    op('pe', lambda: nc.tensor.matmul(ps[pbase:pbase + M, 0:N], lhsT=w[:, kc, c0:c0 + M],
                                                      rhs=hT[:, kc, tok0:tok0 + N], start=(kc == 0), stop=(kc == 7)),
                       (wkey if isinstance(wkey, list) else [wkey]) + hkeys(tok0, N), [pskey], inc=(inc_last and kc == 7))

            def mm_tm(ps, pskey, t, w, wkey, c0, N):
                for kc in range(8):
                    op('pe', lambda: nc.tensor.matmul(ps[:, 0:N], lhsT=hT[:, kc, t * 128:(t + 1) * 128],
                                                      rhs=w[:, kc, c0:c0 + N], start=(kc == 0), stop=(kc == 7)),
                       (wkey if isinstance(wkey, list) else [wkey]) + [('hT', t)], [pskey], inc=(kc == 7))

            groups = [(g * 512, 512) for g in range(4)] + [(T, CTX)]
            W3K = [('W3', 0), ('W3', 1), ('W3', 2), ('W3', 3)]

            def load_kqv(l2, ck, cq, cv, gk, gq, gv):
                S.dma('sp', W3[:, :, 0:384], wview(wb_in[l2])[:, :, ck:ck + 384], reads=WBI(l2, gk), writes=[W3K[0]])
                S.dma('sp', W3[:, :, 384:768], wview(wb_in[l2])[:, :, cq:cq + 384], reads=WBI(l2, gq), writes=[W3K[1]])
                S.dma('sp', W3[:, :, 768:1152], wview(wb_in[l2])[:, :, cv:cv + 384], reads=WBI(l2, gv), writes=[W3K[2]])

            with contextlib.ExitStack() as st:
                def sb2(name, shape, dt):
                    return st.enter_context(nc.sbuf_tensor("s_%s_%d" % (name, l), list(shape), dt))
                wk, wq, wv = W3[:, :, 0:384], W3[:, :, 384:768], W3[:, :, 768:1152]
                kT = sb2("kT", [128, 3, NTOK], BF16)
                qT = sb2("qT", [128, 3, NTOK], BF16)
                Vn = sb2("Vn", [128, NT, 6, 65], BF16)
                btab = sb2("btab", [128, 6, NBLK * 64], BF16)
                bval = [sb2("bval%d" % i, [128, NBLK * 64], F32) for i in range(1)]
                maskt = sb2("maskt", [128, NBLK * 64], F32)
                scf2 = [sb2("scf%d" % i, [128, 2, 512], F32) for i in range(2)]
                pT2 = [sb2("pT%d" % i, [128, 2, 512], BF16) for i in range(3)]
                rc = [sb2("rc%d" % i, [128, 4], F32) for i in range(4)]

                if l == 0:
                    load_kqv(l, C_NAK, C_NAQ, C_NAV, 0, 1, 0)
                S.dma('sp', maskt[:], nab_mask[:, :], writes=['maskt'])
                op('pool', lambda: nc.gpsimd.memset(Vn[:], 1.0), [], ['Vn'])
                for h in range(6):
                    S.dma('sp', bval[0][:], nab_val[l, h, :, :], writes=[('bval', 0)])
                    op('pool', lambda: nc.gpsimd.tensor_tensor(btab[:, h, :], bval[0][:], maskt[:], ALU.add),
                       [('bval', 0), 'maskt'], [('btab', h)])
                if l == 0:
                    pump_casts(0, 100)
                it = 0
                for c in range(3):
                    for gi, (tok0, N) in enumerate(groups):
                        ps = psf[it % 2]
                        mm_fm(ps, PF[it % 2], 0, 128, wk, W3K[0], c * 128, tok0, N)
                        evac_copy(kT[:, c, tok0:tok0 + N], ps[:, 0:N], [PF[it % 2]], [('kT', c, gi)])
                        it += 1
                        if gi == 4 and not upd:
                            continue
                        ps = psf[it % 2]
                        mm_fm(ps, PF[it % 2], 0, 128, wq, W3K[1], c * 128, tok0, N)
                        evac_copy(qT[:, c, tok0:tok0 + N], ps[:, 0:N], [PF[it % 2]], [('qT', c, gi)], scale=0.125)
                        it += 1
                for t in range(NT):
                    ps = psf[it % 2]
                    mm_tm(ps, PF[it % 2], t, wv, W3K[2], 0, 384)
                    evac_copy(Vn[:, t, :, 0:64], ps[:, 0:384].rearrange("p (h d) -> p h d", d=64),
                              [PF[it % 2], 'Vn'], [('Vn', t)])
                    it += 1
                S.dma('sp', W3[:, :, 0:512], wview(wbp[l])[:, :, 0:512], reads=WBI(l, 3), writes=[W3K[0], W3K[1]])
                S.dma('sp', W3[:, :, 512:1024], wview(wbp[l])[:, :, 512:1024], reads=WBI(l, 3), writes=[W3K[1], W3K[2]])
                S.dma('sp', W3[:, :, 1024:1408], wview(wb_in[l])[:, :, C_DFV:C_DFV + 384], reads=WBI(l, 2), writes=[W3K[2], W3K[3]])
                Sb = [psf[0], psf[1], psT[0][:].bitcast(F32), psT[1][:].bitcast(F32)]
                Sk = [PF[0], PF[1], PT[0], PT[1]]
                units = []
                for c in range(3):
                    for G in range(5 if upd else 4):
                        if G < 4:
                            qt0, nq = 4 * G, 4
                            plan = []
                            for j in range(16):
                                rws = [r for r in _na_rows(j) if 8 * G <= r < 8 * G + 8]
                                if rws:
                                    assert rws[0] % 2 == 0 and len(rws) % 2 == 0
                                    plan.append((j, rws))
                            plan += [(16, None), (17, None)]
                        else:
                            qt0, nq = 16, 2
                            plan = [(16, None), (17, None)]
                        gid = c * 5 + G
                        for pi, (j, rws) in enumerate(plan):
                            units.append(dict(c=c, G=G, qt0=qt0, nq=nq, j=j, rws=rws, gid=gid,
                                              first=(pi == 0), last=(pi == len(plan) - 1)))
                DLAG = 2

                def na_front(ui):
                    u = units[ui]
                    c, j, rws = u['c'], u['j'], u['rws']
                    if rws is None:
                        i0, n_i = u['qt0'], u['nq']
                    else:
                        i0, n_i = rws[0] // 2, len(rws) // 2
                    N = n_i * 128
                    u['i0'], u['n_i'], u['N'] = i0, n_i, N
                    gq = min(i0 * 128 // 512, 4)
                    for hp in range(2):
                        pb = hp * 64
                        ps, psk = Sb[2 * (ui % 2) + hp], Sk[2 * (ui % 2) + hp]
                        op('pe', lambda: nc.tensor.matmul(ps[:, 0:N], lhsT=kT[pb:pb + 64, c, j * 128:(j + 1) * 128],
                                                          rhs=qT[pb:pb + 64, c, i0 * 128:i0 * 128 + N],
                                                          start=True, stop=True),
                           [('kT', c, min(j // 4, 4)), ('qT', c, gq)], [psk])
                    pk = ('pT', ui % 3)
                    pout = pT2[ui % 3][:, :, 0:N]
                    pskeys = [Sk[2 * (ui % 2)], Sk[2 * (ui % 2) + 1]]
                    if rws is not None:
                        sfk = ('scf', ui % 2)
                        for hp in range(2):
                            h = 2 * c + hp
                            ps, psk = Sb[2 * (ui % 2) + hp], Sk[2 * (ui % 2) + hp]
                            sf = scf2[ui % 2][:, hp, :]
                            segs = []
                            for r in rws:
                                bb = _na_block(j, r)
                                if segs and segs[-1][1] + segs[-1][2] == bb:
                                    segs[-1][2] += 1
                                else:
                                    segs.append([r, bb, 1])
                            for (r0, b0, nb) in segs:
                                o0 = (r0 - rws[0]) * 64
                                op('dve', lambda: nc.vector.tensor_tensor(sf[:, o0:o0 + nb * 64], ps[:, o0:o0 + nb * 64],
                                                                          btab[:, h, b0 * 64:(b0 + nb) * 64], ALU.add),
                                   [psk, ('btab', h)], [sfk])
                        op('act', lambda: nc.scalar.activation(pout, scf2[ui % 2][:, :, 0:N], AF.Exp), [sfk], [pk])
                    else:
                        pin = PP[1 - (ui % 2)][:].rearrange("p (m n) -> p m n", m=2)[:, :, 0:N]
                        op('act', lambda: nc.scalar.activation(pout, pin, AF.Exp), pskeys, [pk])

                def na_back(ui):
                    u = units[ui]
                    c, j, G, qt0, nq = u['c'], u['j'], u['G'], u['qt0'], u['nq']
                    i0, n_i = u['i0'], u['n_i']
                    for hp in range(2):
                        h = 2 * c + hp
                        p_t, pk = pT2[ui % 3][:, hp, :], ('pT', ui % 3)
                        ai = 2 + 2 * (u['gid'] % 2) + hp
                        acc, acck = psf[ai], PF[ai]
                        for ii in range(n_i):
                            slot = i0 + ii - qt0
                            op('pe', lambda: nc.tensor.matmul(acc[:, slot * 65:(slot + 1) * 65],
                                                              lhsT=p_t[:, ii * 128:(ii + 1) * 128],
                                                              rhs=Vn[:, j, h, :], start=(u['first'] and ii == 0), stop=True,
                                                              skip_group_check=True),
                               [pk, 'Vn', ('Vn', j)], [acck])
                    if u['last']:
                        for hp in range(2):
                            h = 2 * c + hp
                            ai = 2 + 2 * (u['gid'] % 2) + hp
                            acc, acck = psf[ai], PF[ai]
                            accv = acc[:, 0:nq * 65].rearrange("p (i d) -> p i d", d=65)
                            r_ = rc[ai - 2]
                            op('dve', lambda: nc.vector.reciprocal(r_[:, 0:nq], accv[:, :, 64]), [acck], [('rc', ai)])
                            op('dve', lambda: nc.vector.tensor_tensor(
                                yin[:, qt0:qt0 + nq, h * 64:(h + 1) * 64], accv[:, :, 0:64],
                                r_[:, 0:nq].unsqueeze(2).broadcast_to([128, nq, 64]), ALU.mult),
                               [acck, ('rc', ai)], [('yin', 0, G)])

                for s in range(len(units) + DLAG):
                    if s < len(units):
                        na_front(s)
                    if s - DLAG >= 0:
                        na_back(s - DLAG)
            S.barrier()

            if stop_after == 'S2':
                break
            with contextlib.ExitStack() as st:
                def sb2(name, shape, dt):
                    return st.enter_context(nc.sbuf_tensor("s_%s_%d" % (name, l), list(shape), dt))
                wk, wq, wv = W3[:, :, 0:512], W3[:, :, 512:1024], W3[:, :, 1024:1408]
                kT = sb2("dkT", [128, 4, NTOK], BF16)
                qT = sb2("dqT", [128, 4, NTOK], BF16)
                Vd = sb2("Vd", [128, NT, 4, 97], BF16)
                rc_ = sb2("ropec", [128, T], F32)
                rs_ = sb2("ropes", [128, T], F32)
                xraw = [sb2("xraw%d" % i, [128, 512], BF16) for i in range(2)]
                rt1 = [sb2("rt1_%d" % i, [128, 512], F32) for i in range(1)]
                rt2 = [sb2("rt2_%d" % i, [128, 512], F32) for i in range(1)]
                pT2 = [sb2("dpT%d" % i, [128, 2, 512], BF16) for i in range(2)]
                lq = sb2("lq", [128, 4, 48], F32)
                lsm = sb2("lsm", [128, 16], F32)
                gsub = sb2("gsub", [128, 96], F32)
                of1 = [sb2("of1_%d" % i, [128, 4, 96], F32) for i in range(2)]
                of2 = [sb2("of2_%d" % i, [128, 4, 96], F32) for i in range(2)]
                sm = [sb2("dsm%d" % i, [128, 8, 4], F32) for i in range(2)]

                S.dma('sp', rc_[:], ropec_in[:, :], writes=['ropec'])
                S.dma('sp', rs_[:], ropes_in[:, :], writes=['ropes'])
                for i in range(4):
                    S.dma('sp', lq[:, i, :], lqk[l, i, :].partition_broadcast(128), writes=['lq'])
                S.dma('sp', gsub[:], subln_g[l, :].partition_broadcast(128), writes=['gsub0'])
                op('pool', lambda: nc.gpsimd.memset(Vd[:], 1.0), [], ['Vd'])
                for i in range(2):
                    op('pool', lambda: nc.gpsimd.memset(xraw[i][:], 0.0), [], [('xraw', i)])
                op('dve', lambda: nc.vector.tensor_tensor(lq[:, 0, :], lq[:, 0, :], lq[:, 1, :], ALU.mult), ['lq'], ['lq'])
                op('dve', lambda: nc.vector.tensor_tensor(lq[:, 2, :], lq[:, 2, :], lq[:, 3, :], ALU.mult), ['lq'], ['lq'])
                op('dve', lambda: nc.vector.tensor_reduce(lsm[:, 0:1], lq[:, 0, :], AX.X, ALU.add), ['lq'], ['lsm'])
                op('dve', lambda: nc.vector.tensor_reduce(lsm[:, 1:2], lq[:, 2, :], AX.X, ALU.add), ['lq'], ['lsm'])
                op('act', lambda: nc.scalar.activation(lsm[:, 2:4], lsm[:, 0:2], AF.Exp), ['lsm'], ['lsm2'])
                op('dve', lambda: nc.vector.tensor_tensor(lsm[:, 4:5], lsm[:, 3:4], lsm[:, 2:3], ALU.subtract),
                   ['lsm2'], ['lsm3'])
                op('dve', lambda: nc.vector.tensor_scalar(lsm[:, 5:6], lsm[:, 4:5], -lam_init, None, ALU.add),
                   ['lsm3'], ['nlam'])
                op('dve', lambda: nc.vector.tensor_scalar(gsub[:], gsub[:], 1.0 - lam_init, None, ALU.mult),
                   ['gsub0'], ['gsub'])
                nlam = lsm[:, 5:6]

                it = 0
                pend = []

                def rope_back(it_, dst, dkey, c, gi, tok0, N):
                    xr = xraw[it_ % 2]
                    xk = ('xraw', it_ % 2)
                    ps2 = psf[2 + (it_ % 2)]
                    ps2k = PF[2 + (it_ % 2)]
                    op('pe', lambda: nc.tensor.matmul(ps2[:, 0:N], lhsT=permb[:], rhs=xr[:, 0:N],
                                                      start=True, stop=True), ['permb', xk], [ps2k])
                    t1 = rt1[0]
                    t2 = rt2[0]
                    op('pool', lambda: nc.gpsimd.tensor_tensor(t1[:, 0:N], xr[:, 0:N], rc_[:, tok0:tok0 + N], ALU.mult),
                       [xk, 'ropec'], [('rt1', 0)])
                    op('dve', lambda: nc.vector.tensor_tensor(t2[:, 0:N], ps2[:, 0:N], rs_[:, tok0:tok0 + N], ALU.mult),
                       [ps2k, 'ropes'], [('rt2', 0)])
                    op('dve', lambda: nc.vector.tensor_tensor(dst[:, c, tok0:tok0 + N], t1[:, 0:N], t2[:, 0:N], ALU.add),
                       [('rt1', 0), ('rt2', 0)], [(dkey, c, gi)])

                for (w, wkey, dst, dkey, cbase) in ((wk, [W3K[0], W3K[1]], kT, 'kT', 0), (wq, [W3K[1], W3K[2]], qT, 'qT', 0)):
                    for c in range(4):
                        for gi, (tok0, N) in enumerate(groups):
                            if dkey == 'qT' and gi == 4 and not upd:
                                continue
                            ps = psf[it % 2]
                            psk = PF[it % 2]
                            mm_fm(ps, psk, 0, 128, w, wkey, c * 128, tok0, N)
                            if gi == 4:
                                evac_copy(dst[:, c, tok0:tok0 + N], ps[:, 0:N], [psk], [(dkey, c, gi)])
                            else:
                                xr = xraw[it % 2]
                                xk = ('xraw', it % 2)
                                evac_copy(xr[:, 0:N], ps[:, 0:N], [psk], [xk])
                                pend.append((it, dst, dkey, c, gi, tok0, N))
                            if LAGR == 0:
                                while pend:
                                    rope_back(*pend.pop(0))
                            elif len(pend) > 1 or (pend and pend[0][0] < it):
                                rope_back(*pend.pop(0))
                            it += 1
                while pend:
                    rope_back(*pend.pop(0))
                for t in range(NT):
                    ps = psf[it % 2]
                    mm_tm(ps, PF[it % 2], t, wv, [W3K[2], W3K[3]], 0, 384)
                    evac_copy(Vd[:, t, :, 0:96], ps[:, 0:384].rearrange("p (h d) -> p h d", d=96),
                              [PF[it % 2], 'Vd'], [('Vd', t)])
                    it += 1
                S.dma('sp', W3[:, :, 0:512], wview(wb_in[l])[:, :, C_CVA:C_CVA + 512], reads=WBI(l, 4),
                      writes=[W3K[0], W3K[1]])
                dscale = 48.0 ** -0.5
                Sb = [psf[0], psf[1], psT[0][:].bitcast(F32), psT[1][:].bitcast(F32)]
                Sk = [PF[0], PF[1], PT[0], PT[1]]
                units = []
                for hd in range(4):
                    for G in range(5 if upd else 4):
                        if G < 4:
                            qt0, nq, kts = 4 * G, 4, list(range(NT))
                        else:
                            qt0, nq, kts = 16, 2, [16, 17]
                        gid = hd * 5 + G
                        for ki, kt in enumerate(kts):
                            units.append(dict(hd=hd, G=G, qt0=qt0, nq=nq, kt=kt, gid=gid,
                                              first=(ki == 0), last=(ki == len(kts) - 1)))
                DLAG = 1

                def df_front(ui):
                    u = units[ui]
                    hd, G, qt0, nq, kt = u['hd'], u['G'], u['qt0'], u['nq'], u['kt']
                    N = nq * 128
                    for mm in range(2):
                        pb = mm * 64
                        ps, psk = Sb[2 * (ui % 2) + mm], Sk[2 * (ui % 2) + mm]
                        op('pe', lambda: nc.tensor.matmul(ps[:, 0:N], lhsT=kT[pb:pb + 48, hd, kt * 128:(kt + 1) * 128],
                                                          rhs=qT[pb:pb + 48, hd, qt0 * 128:qt0 * 128 + N],
                                                          start=True, stop=True),
                           [('kT', hd, min(kt // 4, 4)), ('qT', hd, G)], [psk])
                    pin = PP[1 - (ui % 2)][:].rearrange("p (m n) -> p m n", m=2)[:, :, 0:N]
                    op('act', lambda: nc.scalar.activation(pT2[ui % 2][:, :, 0:N], pin, AF.Exp, scale=dscale),
                       [Sk[2 * (ui % 2)], Sk[2 * (ui % 2) + 1]], [('pT', ui % 2)])

                def df_back(ui):
                    u = units[ui]
                    hd, G, qt0, nq, kt = u['hd'], u['G'], u['qt0'], u['nq'], u['kt']
                    gb = u['gid'] % 2
                    accs = [psf[2 + 2 * gb + mm] for mm in range(2)]
                    acck = [PF[2 + 2 * gb + mm] for mm in range(2)]
                    for mm in range(2):
                        for ii in range(nq):
                            op('pe', lambda: nc.tensor.matmul(accs[mm][:, ii * 97:(ii + 1) * 97],
                                                              lhsT=pT2[ui % 2][:, mm, ii * 128:(ii + 1) * 128],
                                                              rhs=Vd[:, kt, hd, :], start=(u['first'] and ii == 0), stop=True,
                                                              skip_group_check=True),
                               [('pT', ui % 2), 'Vd', ('Vd', kt)], [acck[mm]])
                    if not u['last']:
                        return
                    b = gb
                    a1 = accs[0][:, 0:nq * 97].rearrange("p (i d) -> p i d", d=97)
                    a2 = accs[1][:, 0:nq * 97].rearrange("p (i d) -> p i d", d=97)
                    s_ = sm[b]
                    o1 = of1[b][:, 0:nq, :]
                    o2 = of2[b][:, 0:nq, :]
                    K1, K2, KS, KO1, KO2 = acck[0], acck[1], ('dsm', b), ('of1', b), ('of2', b)

                    def bc(ap2):
                        return ap2.unsqueeze(2).broadcast_to([128, nq, 96])
                    op('dve', lambda: nc.vector.reciprocal(s_[:, 0, 0:nq], a1[:, :, 96]), [K1], [KS])
                    op('dve', lambda: nc.vector.reciprocal(s_[:, 1, 0:nq], a2[:, :, 96]), [K2], [KS])
                    op('dve', lambda: nc.vector.tensor_scalar(s_[:, 2, 0:nq], s_[:, 1, 0:nq], nlam, None, ALU.mult),
                       [KS, 'nlam'], [KS])
                    op('dve', lambda: nc.vector.tensor_tensor(o1, a1[:, :, 0:96], bc(s_[:, 0, 0:nq]), ALU.mult),
                       [K1, KS], [KO1])
                    op('dve', lambda: nc.vector.tensor_tensor(o2, a2[:, :, 0:96], bc(s_[:, 2, 0:nq]), ALU.mult),
                       [K2, KS], [KO2])
                    op('dve', lambda: nc.vector.tensor_tensor(o1, o1, o2, ALU.add), [KO1, KO2], [KO1])
                    op('dve', lambda: nc.vector.tensor_tensor(o2, o1, o1, ALU.mult), [KO1], [KO2])
                    op('dve', lambda: nc.vector.tensor_reduce(s_[:, 3, 0:nq], o2, AX.X, ALU.add), [KO2], [KS])
                    op('dve', lambda: nc.vector.tensor_scalar(s_[:, 4, 0:nq], s_[:, 3, 0:nq], 1.0 / 96, EPS,
                                                              ALU.mult, ALU.add), [KS], [KS])
                    op('pool', lambda: nc.gpsimd.tensor_tensor(s_[:, 6, 0:nq], s_[:, 4, 0:nq], nhalf[:, 0:nq], ALU.pow),
                       [KS, 'nhalf'], [KS])
                    op('dve', lambda: nc.vector.tensor_tensor(o2, o1, bc(s_[:, 6, 0:nq]), ALU.mult), [KO1, KS], [KO2])
                    op('dve', lambda: nc.vector.tensor_tensor(
                        yin[:, qt0:qt0 + nq, 384 + hd * 96:384 + (hd + 1) * 96], o2,
                        gsub[:].unsqueeze(1).broadcast_to([128, nq, 96]), ALU.mult),
                       [KO2, 'gsub'], [('yin', 1, G)])

                for s in range(len(units) + DLAG):
                    if s < len(units):
                        df_front(s)
                    if s - DLAG >= 0:
                        df_back(s - DLAG)
                    if s % 4 == 1:
                        pump_casts(l + 1, 1)
                pump_casts(l + 1, 100)
            S.barrier()

            if stop_after == 'S3':
                break
            with contextlib.ExitStack() as st:
                def sb2(name, shape, dt):
                    return st.enter_context(nc.sbuf_tensor("s_%s_%d" % (name, l), list(shape), dt))
                wab = W3[:, :, 0:512]
                uT = sb2("uT", [128, 2, U_LEN], BF16)
                cw = sb2("cw", [128, 2, 31], F32)
                cb = sb2("cb", [128, 2], F32)
                diag = sb2("diag", [128, 2, 31, 128], BF16)
                lng = sb2("lng", [128, 256], F32)
                lnb = sb2("lnb", [128, 256], F32)
                sig = [sb2("sig%d" % i, [128, 512], F32) for i in range(2)]
                ycs = [sb2("ycs%d" % i, [128, 512], F32) for i in range(4)]
                lst = [sb2("lst%d" % i, [128, 16], F32) for i in range(2)]
                yn = [sb2("yn%d" % i, [128, 256], F32) for i in range(2)]

                S.dma('sp', cw[:], conv_wT[l].rearrange("(cc p) k -> p cc k", p=128), writes=['cw'])
                for cc in range(2):
                    S.dma('sp', cb[:, cc:cc + 1], conv_b[l, cc * 128:(cc + 1) * 128].rearrange("(p o) -> p o", o=1), writes=['cb'])
                S.dma('sp', lng[:], ln_g[l, :].partition_broadcast(128), writes=['lng'])
                S.dma('sp', lnb[:], ln_b[l, :].partition_broadcast(128), writes=['lnb'])
                op('pool', lambda: nc.gpsimd.memset(uT[:], 0.0), [], ['uT'])
                cgroups = groups if upd else groups[:4]
                nxt = (l + 1 < n_layers)
                if nxt:
                    bufs1 = s0a_bufs(st, "b")
                    s0a_load(l + 1, 0, *bufs1, 0)
                    s0a_load(l + 1, 1, *bufs1, 1)
                s0n = [0]

                def s0a_step():
                    if nxt and s0n[0] < 6:
                        n_ = s0n[0]
                        s0a_compute(l + 1, n_, *bufs1, n_ % 2)
                        if n_ + 2 < 6:
                            s0a_load(l + 1, n_ + 2, *bufs1, n_ % 2)
                        s0n[0] += 1
                it = 0
                for cc in range(2):
                    for gi, (tok0, N) in enumerate(cgroups):
                        if it % 2 == 1:
                            s0a_step()
                        u0 = (U_LAT0 + tok0) if gi < 4 else U_CTX0
                        psa, psak = psf[0], PF[0]
                        psb, psbk = psf[1], PF[1]
                        mm_fm(psa, psak, 0, 128, wab, [W3K[0], W3K[1]], cc * 128, tok0, N)
                        mm_fm(psb, psbk, 0, 128, wab, [W3K[0], W3K[1]], 256 + cc * 128, tok0, N)
                        sg = sig[it % 2]
                        op('act', lambda: nc.scalar.activation(sg[:, 0:N], psb[:, 0:N], AF.Sigmoid), [psbk], [('sig', it % 2)])
                        op('dve', lambda: nc.vector.tensor_tensor(uT[:, cc, u0:u0 + N], psa[:, 0:N], sg[:, 0:N], ALU.mult),
                           [psak, ('sig', it % 2), 'uT'], [('uT', cc, gi)])
                        it += 1
                for cc in range(2):
                    for k in range(31):
                        op('dve', lambda: nc.vector.tensor_scalar(diag[:, cc, k, :], identb[:], cw[:, cc, k:k + 1], None, ALU.mult),
                           ['identb', 'cw'], ['diag'])
                S.dma('sp', W3[:, :, 0:384], wview(wb_in[l])[:, :, C_NAG:C_NAG + 384], reads=WBI(l, 5), writes=[W3K[0]])
                S.dma('sp', W3[:, :, 384:768], wview(wb_in[l])[:, :, C_DFG:C_DFG + 384], reads=WBI(l, 6), writes=[W3K[1]])
                S.dma('sp', W3[:, :, 768:1024], wview(wb_in[l])[:, :, C_CVG:C_CVG + 256], reads=WBI(l, 7), writes=[W3K[2]])
                if l > 0:
                    s0b(l, [2], outer=st)
                cnt_it = [0]

                def cv_X(gi):
                    tok0, N = cgroups[gi]
                    u0 = (U_LAT0 + tok0) if gi < 4 else U_CTX0
                    for cc in range(2):
                        bi = 2 * (gi % 2) + cc
                        ps, psk = psf[bi], PF[bi]
                        ukeys = ['uT'] + [('uT', cc, g2) for g2 in range(len(cgroups)) if abs(g2 - gi) <= 1]
                        for k in range(31):
                            op('pe', lambda: nc.tensor.matmul(ps[:, 0:N], lhsT=diag[:, cc, k, :],
                                                              rhs=uT[:, cc, u0 - UPAD + k:u0 - UPAD + k + N],
                                                              start=(k == 0), stop=(k == 30)),
                               ['diag'] + ukeys, [psk], inc=(k == 30))
                        op('act', lambda: nc.scalar.activation(ycs[bi][:, 0:N], ps[:, 0:N], AF.Identity, bias=cb[:, cc:cc + 1]),
                           [psk, 'cb'], [('ycs', bi)])

                def cv_Y(gi):
                    tok0, N = cgroups[gi]
                    for tt in range(N // 128):
                        t = tok0 // 128 + tt
                        b = cnt_it[0] % 2
                        cnt_it[0] += 1
                        pst, pstk = psf[4 + b], PF[4 + b]
                        for cc in range(2):
                            bi = 2 * (gi % 2) + cc
                            op('pe', lambda: nc.tensor.transpose(pst[:, cc * 128:(cc + 1) * 128],
                                                                 ycs[bi][:, tt * 128:(tt + 1) * 128], identf[:]),
                               [('ycs', bi), 'identf'], [pstk], inc=(cc == 1))
                        ls = lst[b]
                        LK = ('lst', b)
                        op('dve', lambda: nc.vector.bn_stats(ls[:, 0:6], pst[:, 0:256]), [pstk], [LK])
                        op('dve', lambda: nc.vector.bn_aggr(ls[:, 6:8], ls[:, 0:6]), [LK], [LK])
                        op('dve', lambda: nc.vector.tensor_scalar(ls[:, 8:9], ls[:, 7:8], EPS, None, ALU.add), [LK], [LK])
                        op('pool', lambda: nc.gpsimd.tensor_tensor(ls[:, 10:11], ls[:, 8:9], nhalf[:, 0:1], ALU.pow),
                           [LK, 'nhalf'], [LK])
                        y_ = yn[b]
                        YK = ('yn', b)
                        op('dve', lambda: nc.vector.tensor_scalar(y_[:], pst[:, 0:256], ls[:, 6:7], ls[:, 10:11],
                                                                  ALU.subtract, ALU.mult), [pstk, LK], [YK])
                        op('dve', lambda: nc.vector.tensor_tensor(y_[:], y_[:], lng[:], ALU.mult), [YK, 'lng'], [YK])
                        op('dve', lambda: nc.vector.tensor_tensor(y_[:], y_[:], lnb[:], ALU.add), [YK, 'lnb'], [YK])
                        op('act', lambda: nc.scalar.activation(yin[:, t, 768:1024], y_[:], AF.Silu), [YK],
                           [('yin', 2, t)])

                for s in range(len(cgroups) + 1):
                    if s < len(cgroups):
                        cv_X(s)
                    if s - 1 >= 0:
                        cv_Y(s - 1)
                    s0a_step()
                while nxt and s0n[0] < 6:
                    s0a_step()
            S.barrier()

            if stop_after == 'S4':
                break
            if debug:
                for t in range(NT):
                    S.dma('sp', dbg_yin[l, t * 128:(t + 1) * 128, :], yin[:, t, :], reads=[], writes=[('dbgy', t)])

            with contextlib.ExitStack() as st:
                def sb2(name, shape, dt):
                    return st.enter_context(nc.sbuf_tensor("s_%s_%d" % (name, l), list(shape), dt))
                wg = W3[:, :, 0:D]
                wo = sb2("wo", [128, 8, D], BF16)
                xt = [sb2("fxt%d" % i, [128, D], F32) for i in range(2)]
                gs = [sb2("gs%d" % i, [128, D], BF16) for i in range(2)]
                yg = [sb2("yg%d" % i, [128, D], BF16) for i in range(2)]
                yT = [sb2("yT%d" % i, [128, 8, 128], BF16) for i in range(2)]
                tmp = [sb2("ftmp%d" % i, [128, D], F32) for i in range(2)]
                xo = [sb2("xo%d" % i, [128, D], F32) for i in range(2)]
                fst = sb2("fst", [128, NT, 8], F32)
                S.dma('sp', wo[:], wview(wb_out[l])[:, :, :], reads=WBO(l), writes=['wo'])
                nxt = (l + 1 < n_layers)
                if nxt:
                    s0b(l + 1, [0, 1], outer=st)
                    ntmpf = [sb2("ntmpf%d" % i, [128, D], F32) for i in range(2)]
                    nhb = [sb2("nhb%d" % i, [128, D], BF16) for i in range(4)]
                    nstat = sb2("nstat", [128, 4, NT], F32)
                def f_A(t):
                    b = t % 2
                    for half in range(2):
                        ps, psk = psf[half], PF[half]
                        mm_tm(ps, psk, t, wg, W3K, half * 512, 512)
                        op('act', lambda: nc.scalar.activation(gs[b][:, half * 512:(half + 1) * 512], ps[:, :], AF.Silu),
                           [psk], [('gs', b, half)])
                    op('dve', lambda: nc.vector.tensor_tensor(yg[b][:], yin[:, t, :], gs[b][:], ALU.mult),
                       [('gs', b, 0), ('gs', b, 1)], [('yg', b)])

                def f_B(t):
                    b = t % 2
                    S.dma('sp', xt[b][:], Xsrc(t), reads=xreads(t), writes=[('fxt', b)])
                    for kc in range(8):
                        op('pe', lambda: nc.tensor.transpose(psT[b][:, kc * 128:(kc + 1) * 128],
                                                             yg[b][:, kc * 128:(kc + 1) * 128], identb[:]),
                           [('yg', b), 'identb'], [PT[b]], inc=(kc == 7))
                    evac_copy(yT[b][:], psT[b][:].rearrange("p (a b) -> p a b", b=128), [PT[b]], [('yT', b)])

                def f_C(t):
                    b = t % 2
                    which = 0 if t < 16 else 1
                    f = fst[:, t, :]
                    FK = ('fst', t)
                    for half in range(2):
                        ps, psk = psf[2 + 2 * b + half], PF[2 + 2 * b + half]
                        for kc in range(8):
                            op('pe', lambda: nc.tensor.matmul(ps[:, :], lhsT=yT[b][:, kc, :],
                                                              rhs=wo[:, kc, half * 512:(half + 1) * 512],
                                                              start=(kc == 0), stop=(kc == 7)),
                               [('yT', b), 'wo'], [psk], inc=(kc == 7))
                        op('act', lambda: nc.scalar.activation(junk[:, 0:512], ps[:, :], AF.Square, accum_out=f[:, half:half + 1]),
                           [psk], ['junk', (FK, half)])
                        op('dve', lambda: nc.vector.tensor_tensor(tmp[b][:, half * 512:(half + 1) * 512], ps[:, :],
                                                                  tabG[:, which, half * 512:(half + 1) * 512], ALU.mult),
                           [psk], [('ftmp', b, half)])
                    op('dve', lambda: nc.vector.tensor_tensor(f[:, 2:3], f[:, 0:1], f[:, 1:2], ALU.add),
                       [(FK, 0), (FK, 1)], [FK])
                    op('dve', lambda: nc.vector.tensor_scalar(f[:, 3:4], f[:, 2:3], 1.0 / D, EPS, ALU.mult, ALU.add), [FK], [FK])
                    op('pool', lambda: nc.gpsimd.tensor_tensor(f[:, 5:6], f[:, 3:4], nhalf[:, 0:1], ALU.pow),
                       [FK, 'nhalf'], [FK])
                    op('dve', lambda: nc.vector.scalar_tensor_tensor(xo[b][:], tmp[b][:], f[:, 5:6], xt[b][:], ALU.mult, ALU.add),
                       [('ftmp', b, 0), ('ftmp', b, 1), FK, ('fxt', b)], [('xo', b)])
                    if last:
                        S.dma('sp', out[t * 128:(t + 1) * 128, :], xo[b][:], reads=[('xo', b)], writes=[('out', t)])
                    else:
                        S.dma('sp', Xs[t * 128:(t + 1) * 128, :], xo[b][:], reads=[('xo', b)], writes=[('Xs', t)])
                    if debug:
                        S.dma('sp', dbg_x[l, t * 128:(t + 1) * 128, :], xo[b][:], reads=[('xo', b)], writes=[('dbgx', t)])
                    if nxt:
                        prenorm_front(t, xo[b][:], ('xo', b), (ntmpf[b], ('ntmpf', b)), (nhb[t % 4], ('nhb', t % 4)), nstat)

                def f_D(t):
                    prenorm_back(t, (nhb[t % 4], ('nhb', t % 4)), t % 2)

                for s in range(ntile_q + 5):
                    if s < ntile_q:
                        f_A(s)
                    if 0 <= s - 1 < ntile_q:
                        f_B(s - 1)
                    if nxt and 0 <= s - LAGD < ntile_q:
                        f_D(s - LAGD)
                    if 0 <= s - 2 < ntile_q:
                        f_C(s - 2)
                    if s == ntile_q - 1 and nxt:
                        load_kqv(l + 1, C_NAK, C_NAQ, C_NAV, 0, 1, 0)
            S.barrier()

        if debug and stop_after is not None:
            for t in range(NT):
                S.dma('sp', dbg_yin[0, t * 128:(t + 1) * 128, :], yin[:, t, :], reads=[], writes=[('dbgy', t)])
        S.finish()
    return nc


def _host_constants():
    ident = np.eye(128, dtype=np.float32)
    selc = np.zeros((2, 2, 128), np.float32)
    selc[0, 0, :] = 1.0
    selc[1, 1, :] = 1.0
    t = np.arange(T)
    rows = (t // 64).astype(np.float32)
    cols = (t % 64).astype(np.float32)
    inv = (10000.0 ** (-np.arange(0, 24, 2, dtype=np.float32) / 24.0)).astype(np.float32)
    ropec = np.zeros((128, T), np.float32)
    ropes = np.zeros((128, T), np.float32)
    perm = np.zeros((128, 128), np.float32)
    for mm in range(2):
        for d in range(48):
            p = mm * 64 + d
            pos = rows if d < 24 else cols
            dd = d % 24
            i = dd % 12
            ang = (pos * inv[i]).astype(np.float32)
            ropec[p] = np.cos(ang)
            s = np.sin(ang)
            ropes[p] = -s if dd < 12 else s
            partner = d + 12 if dd < 12 else d - 12
            perm[mm * 64 + partner, p] = 1.0
    kr_l = np.arange(128) // 64
    kc = np.arange(128) % 64
    qc = np.arange(64)
    cstart = np.clip(qc - 8, 0, 48)
    colvalid = (kc[:, None] >= cstart[None, :]) & (kc[:, None] < cstart[None, :] + 16)
    dc_idx = np.clip(kc[:, None] - qc[None, :] + 15, 0, 30)
    dr_idx = np.zeros((128, NBLK * 64), np.int64)
    dc_all = np.zeros((128, NBLK * 64), np.int64)
    mask = np.zeros((128, NBLK * 64), np.float32)
    for b in range(NBLK):
        e = 6 - b if b < 14 else 4 - (b - 14)
        dr = e + kr_l
        rowvalid = np.ones(128, bool) if b < 14 else ((dr >= -4) & (dr <= 3))
        valid = colvalid & rowvalid[:, None]
        sl = slice(b * 64, (b + 1) * 64)
        dr_idx[:, sl] = np.clip(dr + 7, 0, 14)[:, None]
        dc_all[:, sl] = dc_idx
        mask[:, sl] = np.where(valid, 0.0, NEG)
    return dict(ident=ident, selc=selc, ropec=ropec, ropes=ropes, perm=perm,
                nab_mask=mask, zpad=np.zeros((1024, 64), np.float32)), dr_idx, dc_all


_NC_CACHE = {}


def _prep_inputs(inputs, batch_ids):
    consts, dr_idx, dc_all = _host_constants()
    f = lambda a: np.ascontiguousarray(np.asarray(a, dtype=np.float32))
    rpb = f(inputs['na_rpb'])
    nab_val = np.ascontiguousarray(rpb[:, :, dr_idx, dc_all])
    lqk = np.ascontiguousarray(np.stack([f(inputs['diff_lq1']), f(inputs['diff_lk1']),
                                         f(inputs['diff_lq2']), f(inputs['diff_lk2'])], axis=1))
    shared = dict(
        w_mod=f(inputs['w_mod']), b_mod=f(inputs['b_mod']), g_pre=f(inputs['g_pre']), g_post=f(inputs['g_post']),
        w_in=f(inputs['w_in']), w_out=f(inputs['w_out']), nab_val=nab_val, lqk=lqk,
        subln_g=f(inputs['diff_subln_g']), conv_wT=np.ascontiguousarray(f(inputs['conv_w']).transpose(0, 2, 1)),
        conv_b=f(inputs['conv_b']), ln_g=f(inputs['conv_ln_g']), ln_b=f(inputs['conv_ln_b']),
        cctxT=np.ascontiguousarray(f(inputs['c_ctx']).reshape(8, 128).T),
    )
    shared.update(consts)
    x = f(inputs['x'])
    ctx = f(inputs['ctx'])
    c = f(inputs['c'])
    maps = []
    for b in batch_ids:
        m = dict(shared)
        m['x'] = x[b]
        m['ctx'] = ctx[b]
        m['cT'] = np.ascontiguousarray(c[b].reshape(8, 128).T)
        maps.append(m)
    return maps


def kernel(**inputs):
    if 'full' not in _NC_CACHE:
        _NC_CACHE['full'] = build_nc(DEPTH, debug=False)
    nc = _NC_CACHE['full']
    maps = _prep_inputs(inputs, list(range(8)))
    res = run_bass_kernel_spmd(nc, maps, core_ids=list(range(8)))
    return np.stack([np.asarray(r['out'], dtype=np.float32) for r in res.results], axis=0)
```
